# Optimizing a Trainium2 kernel written in Bass

```python
import math
import jax, jax.numpy as jnp
from jax import lax
import numpy as np

D_MODEL = 1024
BATCH = 2
SEQ = 8192
DEPTH = 2

N_Q_HEADS = 16
N_KV_HEADS = 2
HEAD_DIM = 64
Q_GROUP = N_Q_HEADS // N_KV_HEADS
ATTN_WIDTH = N_Q_HEADS * HEAD_DIM
KV_WIDTH = N_KV_HEADS * HEAD_DIM
WINDOW = 128
BLOCK = 128
NEG_INF = -1e30
NUM_BUCKETS = 32
MAX_DISTANCE = 128
POOL_WINDOWS = (2, 4, 8, 16)
N_POOL_GROUPS = 4
POOL_WIDTH = D_MODEL
POOL_GROUP_WIDTH = POOL_WIDTH // N_POOL_GROUPS
DEEPNORM_ALPHA = (2 * DEPTH) ** 0.25
DEEPNORM_BETA = (8 * DEPTH) ** -0.25
LN_EPS = 1e-5
SPLITS = (ATTN_WIDTH, KV_WIDTH, KV_WIDTH, ATTN_WIDTH, POOL_WIDTH, POOL_WIDTH, D_MODEL, D_MODEL)
IN_WIDTH = sum(SPLITS)

kernel_name = "hybrid_swa_sink_pool_deepnorm_adaln"


def layer_norm(x, gain, bias):
    xf = x.astype(jnp.float32)
    mu = jnp.mean(xf, axis=-1, keepdims=True)
    var = jnp.mean(jnp.square(xf - mu), axis=-1, keepdims=True)
    y = (xf - mu) * lax.rsqrt(var + LN_EPS)
    return (y * gain.astype(jnp.float32) + bias.astype(jnp.float32)).astype(x.dtype)


def t5_causal_bucket(dist):
    max_exact = NUM_BUCKETS // 2
    is_small = dist < max_exact
    d = jnp.maximum(dist, 1).astype(jnp.float32)
    large = max_exact + (jnp.log(d / max_exact) / math.log(MAX_DISTANCE / max_exact)
                         * (NUM_BUCKETS - max_exact)).astype(jnp.int32)
    large = jnp.minimum(large, NUM_BUCKETS - 1)
    return jnp.where(is_small, dist, large)


def band_structure(n_blocks, rel_bias):
    q_loc = jnp.arange(BLOCK)[:, None]
    k_loc = jnp.arange(2 * BLOCK)[None, :]
    dist = q_loc + BLOCK - k_loc
    in_window = (dist >= 0) & (dist < WINDOW)
    key_exists = (jnp.arange(n_blocks)[:, None, None] > 0) | (k_loc[None] >= BLOCK)
    mask = in_window[None] & key_exists
    bucket = t5_causal_bucket(jnp.maximum(dist, 0))
    bias = rel_bias.astype(jnp.float32)[bucket]
    bias = jnp.transpose(bias, (2, 0, 1)).reshape(N_KV_HEADS, Q_GROUP, BLOCK, 2 * BLOCK)
    return mask, bias


def sliding_window_sink_attention(q, k, v, sinks, bias, mask):
    B, S, _ = q.shape
    nb = S // BLOCK
    qb = q.reshape(B, nb, BLOCK, N_KV_HEADS, Q_GROUP, HEAD_DIM)

    def band(t):
        t = t.reshape(B, S, N_KV_HEADS, HEAD_DIM)
        t = jnp.pad(t, ((0, 0), (BLOCK, 0), (0, 0), (0, 0)))
        t = t.reshape(B, nb + 1, BLOCK, N_KV_HEADS, HEAD_DIM)
        return jnp.concatenate([t[:, :-1], t[:, 1:]], axis=2)

    kb, vb = band(k), band(v)
    logits = jnp.einsum('bnqhgd,bnkhd->bnhgqk', qb, kb).astype(jnp.float32) * (HEAD_DIM ** -0.5) + bias
    logits = jnp.where(mask[None, :, None, None], logits, NEG_INF)
    sink = jnp.broadcast_to(sinks.astype(jnp.float32).reshape(N_KV_HEADS, Q_GROUP, 1, 1),
                            logits.shape[:-1] + (1,))
    probs = jax.nn.softmax(jnp.concatenate([logits, sink], axis=-1), axis=-1)[..., :-1]
    out = jnp.einsum('bnhgqk,bnkhd->bnqhgd', probs.astype(v.dtype), vb)
    return out.reshape(B, S, ATTN_WIDTH)


def multiscale_causal_pool(u):
    B, S, _ = u.shape
    ug = u.reshape(B, S, N_POOL_GROUPS, POOL_GROUP_WIDTH).astype(jnp.float32)
    cs = jnp.concatenate([jnp.zeros((B, 1, N_POOL_GROUPS, POOL_GROUP_WIDTH), jnp.float32),
                          jnp.cumsum(ug, axis=1)], axis=1)
    outs = []
    for gi, w in enumerate(POOL_WINDOWS):
        csg = cs[:, :, gi]
        lo = jnp.pad(csg, ((0, 0), (w - 1, 0), (0, 0)))[:, :S]
        count = jnp.minimum(jnp.arange(1, S + 1), w).astype(jnp.float32)[None, :, None]
        outs.append((csg[:, 1:] - lo) / count - ug[:, :, gi])
    return jnp.stack(outs, axis=2).astype(u.dtype)


def setup_inputs(seed: int = 0) -> dict:
    key = jax.random.key(seed)
    ks = jax.random.split(key, 16)
    f32 = jnp.float32
    n = lambda k, shape: jax.random.normal(k, shape, f32)
    return {
        "x": n(ks[0], (BATCH, SEQ, D_MODEL)),
        "c": n(ks[1], (BATCH, D_MODEL)),
        "rel_bias": 0.5 * n(ks[2], (NUM_BUCKETS, N_Q_HEADS)),
        "w_ada": 0.1 * D_MODEL ** -0.5 * n(ks[3], (DEPTH, D_MODEL, 3 * D_MODEL)),
        "b_ada": 0.01 * n(ks[4], (DEPTH, 3 * D_MODEL)),
        "w_in": D_MODEL ** -0.5 * n(ks[5], (DEPTH, D_MODEL, IN_WIDTH)),
        "sinks": n(ks[6], (DEPTH, N_Q_HEADS)),
        "w_pool_mix": POOL_GROUP_WIDTH ** -0.5 * n(ks[7], (DEPTH, N_POOL_GROUPS, POOL_GROUP_WIDTH, POOL_GROUP_WIDTH)),
        "pool_scale": 1.0 + 0.1 * n(ks[8], (DEPTH, POOL_WIDTH)),
        "w_attn_proj": DEEPNORM_BETA * ATTN_WIDTH ** -0.5 * n(ks[9], (DEPTH, ATTN_WIDTH, D_MODEL)),
        "w_pool_proj": DEEPNORM_BETA * POOL_WIDTH ** -0.5 * n(ks[10], (DEPTH, POOL_WIDTH, D_MODEL)),
        "w_out": DEEPNORM_BETA * D_MODEL ** -0.5 * n(ks[11], (DEPTH, D_MODEL, D_MODEL)),
        "ln_gain": 1.0 + 0.02 * n(ks[12], (DEPTH, D_MODEL)),
        "ln_bias": 0.02 * n(ks[13], (DEPTH, D_MODEL)),
    }


def reference(x, c, rel_bias, w_ada, b_ada, w_in, sinks, w_pool_mix, pool_scale,
              w_attn_proj, w_pool_proj, w_out, ln_gain, ln_bias):
    B, S, _ = x.shape
    mask, bias = band_structure(S // BLOCK, rel_bias)
    offsets = [int(o) for o in np.cumsum(SPLITS)[:-1]]
    for l in range(DEPTH):
        mod = jax.nn.silu(c) @ w_ada[l] + b_ada[l]
        shift, scale, gate = jnp.split(mod, 3, axis=-1)
        u = x * (1.0 + scale[:, None, :]) + shift[:, None, :]
        h = u @ w_in[l]
        q, k, v, a_gate, p_in, p_gate, m_a, m_p = jnp.split(h, offsets, axis=-1)
        a = sliding_window_sink_attention(q, k, v, sinks[l], bias, mask) * jax.nn.silu(a_gate)
        a = a @ w_attn_proj[l]
        p = jnp.einsum('bsgc,gcd->bsgd', multiscale_causal_pool(p_in), w_pool_mix[l]).reshape(B, S, POOL_WIDTH)
        p = (p * pool_scale[l]) * jax.nn.silu(p_gate)
        p = p @ w_pool_proj[l]
        y = jax.nn.sigmoid(m_a) * a + jax.nn.sigmoid(m_p) * p
        y = (y @ w_out[l]) * (1.0 + gate[:, None, :])
        x = layer_norm(DEEPNORM_ALPHA * x + y, ln_gain[l], ln_bias[l])
    return x
```

```python
import math
import numpy as np
import concourse.bass as bass
import concourse.mybir as mybir
from concourse.bass_utils import run_bass_kernel_spmd

F32 = mybir.dt.float32
BF16 = mybir.dt.bfloat16
AF = mybir.ActivationFunctionType
ALU = mybir.AluOpType

D = 1024
DEPTH = 2
NOWN = 16
ALPHA = (2 * DEPTH) ** 0.25
LN_EPS = 1e-5
EPS2 = LN_EPS / (ALPHA * ALPHA)
GSC = 0.5 / ALPHA
NSLAB = 4
POOL_WINDOWS = (2, 4, 8, 16)
NEG = -1e30


class Prog:
    def __init__(self, nc):
        self.nc = nc
        self.ops = []
        self.last_w = {}
        self.readers = {}
        self.dma_last = {}

    def op(self, eng, fn, reads=(), writes=(), dma=None):
        idx = len(self.ops)
        me = ("dma", dma) if dma is not None else eng
        deps = set()
        for r in reads:
            w = self.last_w.get(r)
            if w is not None:
                deps.add((w, True))
        for r in writes:
            w = self.last_w.get(r)
            if w is not None:
                deps.add((w, False))
            for rd in self.readers.get(r, {}).values():
                deps.add((rd, False))
        if dma is not None and dma in self.dma_last:
            deps.add((self.dma_last[dma], True))
        final = set()
        for (d, raw) in deps:
            de = self.ops[d]["me"]
            if de == me and not isinstance(me, tuple):
                if raw and me != "pe":
                    final.add(d)
            else:
                final.add(d)
        for r in reads:
            self.readers.setdefault(r, {})[me] = idx
        for r in writes:
            self.last_w[r] = idx
            self.readers[r] = {}
        if dma is not None:
            self.dma_last[dma] = idx
        self.ops.append(dict(eng=eng, me=me, fn=fn, deps=final, dma=dma, inc=False))
        return idx

    def alias(self, new_keys, old_keys):
        acc = {}
        for o in old_keys:
            if o in self.last_w:
                w = self.last_w[o]
                k = self.ops[w]["me"]
                acc[k] = max(acc.get(k, -1), w)
            for k, rd in self.readers.get(o, {}).items():
                acc[k] = max(acc.get(k, -1), rd)
        for n in new_keys:
            cur = self.readers.setdefault(n, {})
            for k, v in acc.items():
                cur[k] = max(cur.get(k, -1), v)

    def emit(self):
        nc = self.nc
        ops = self.ops
        for o in ops:
            for d in o["deps"]:
                ops[d]["inc"] = True
        engs = ["pe", "act", "dve", "pool", "sp"]
        esem = {e: nc.alloc_semaphore("sem_" + e) for e in engs}
        dkeys = []
        for o in ops:
            if o["dma"] is not None and o["dma"] not in dkeys:
                dkeys.append(o["dma"])
        dsem = {k: nc.alloc_semaphore("dsem_%d" % i) for i, k in enumerate(dkeys)}
        cnt = {e: 0 for e in engs}
        dcnt = {k: 0 for k in dkeys}
        for o in ops:
            if o["dma"] is not None:
                dcnt[o["dma"]] += 16
                o["sem"] = dsem[o["dma"]]
                o["val"] = dcnt[o["dma"]]
            elif o["inc"]:
                cnt[o["eng"]] += 1
                o["sem"] = esem[o["eng"]]
                o["val"] = cnt[o["eng"]]
        final_dma = dict(dcnt)

        def run(engname):
            def body(e):
                waited = {}
                for o in ops:
                    if o["eng"] != engname:
                        continue
                    for d in sorted(o["deps"]):
                        dd = ops[d]
                        s, v = dd["sem"], dd["val"]
                        if waited.get(s.num if hasattr(s, "num") else id(s), 0) < v:
                            e.wait_ge(s, v)
                            waited[s.num if hasattr(s, "num") else id(s)] = v
                    ins = o["fn"](e)
                    if o["dma"] is not None:
                        ins.then_inc(o["sem"], 16)
                    elif o["inc"]:
                        ins.then_inc(o["sem"], 1)
                if engname == "sp":
                    for k in dkeys:
                        if final_dma[k] > 0:
                            e.wait_ge(dsem[k], final_dma[k])
            return body

        with nc.Block() as block:
            block.tensor(run("pe"))
            block.scalar(run("act"))
            block.vector(run("dve"))
            block.gpsimd(run("pool"))
            block.sync(run("sp"))


def chunks(slots):
    slots = list(slots)
    r = len(slots) % 4
    out = []
    if r:
        out.append(slots[:r])
    for i in range(r, len(slots), 4):
        out.append(slots[i:i + 4])
    return out


class Tile:
    def __init__(self, idx, l):
        self.idx = idx
        self.l = l
        if idx == 0:
            if l == 0:
                self.halo, self.full = 0, [1, 2, 3, 4, 5]
            else:
                self.halo, self.full = 1, [2, 3, 4, 5]
            self.o_of = {s: s - 2 for s in range(6)}
        else:
            self.halo, self.full = None, [1, 2, 3, 4]
            self.o_of = {s: 4 * idx + (s - 1) for s in range(1, 5)}
        if idx == 0:
            self.xp = {s: s for s in range(6)}
        else:
            ring = {1: [0, 1, 2, 3], 2: [4, 5, 0, 1], 3: [2, 3, 4, 5]}[idx]
            self.xp = {s: ring[s - 1] for s in range(1, 5)}
        self.base = self.full[0]
        self.alls = ([self.halo] if self.halo is not None else []) + self.full

    def fcol(self, s):
        return (s - self.base) * 128


def build_program(taps=()):
    nc = bass.Bass("TRN2", target_bir_lowering=False)
    P = Prog(nc)

    def din(name, shape, dt=F32):
        return nc.dram_tensor(name, list(shape), dt, kind="ExternalInput")

    x_in = din("x_in", [18, 128, 1024]).ap()
    c_lay = din("c_lay", [128, 8, 2]).ap()
    w_ada = din("w_ada", [2, 1024, 3072]).ap()
    bada_fm = din("bada_fm", [128, 2, 16]).ap()
    bada_g = din("bada_g", [2, 1024])
    w_in = din("w_in", [2, 1024, 6400]).ap()
    w_ap = din("w_ap", [2, 1024, 1024]).ap()
    w_pp = din("w_pp", [2, 1024, 1024]).ap()
    w_out = din("w_out", [2, 1024, 1024]).ap()
    w_mix = din("w_mix", [2, 4, 256, 256]).ap()
    pscale_fm = din("pscale_fm", [128, 2, 8]).ap()
    sinks_d = din("sinks", [2, 16])
    ln_g = din("ln_g", [2, 1024])
    ln_b = din("ln_b", [2, 1024])
    rbg = din("rbg", [128, 16, 256]).ap()
    msk = din("msk", [128, 256]).ap()
    bands_d = din("bands", [3, 4, 128, 128]).ap()
    blkvalid = din("blkvalid", [128, 2]).ap()
    ident_d = din("ident", [128, 128]).ap()
    i8_d = din("i8", [128, 128]).ap()
    out_d = nc.dram_tensor("out", [16, 128, 1024], F32, kind="ExternalOutput").ap()

    sb = nc.alloc_sbuf_tensor
    X = sb("X", [128, 6, 1024], F32)
    UT = sb("UT", [128, 8, 768], BF16)
    B0 = sb("B0", [128, 5120], BF16)
    B1 = sb("B1", [128, 5120], BF16)
    B2 = sb("B2", [128, 5120], BF16)
    PIN = sb("PIN", [128, 6, 1024], BF16)
    PINPREV = sb("PINPREV", [128, 2, 1024], BF16)
    KT4 = sb("KT4", [128, 4, 768], BF16)
    KTPREV = sb("KTPREV", [128, 2, 4, 128], BF16)
    V1 = sb("V1", [128, 6, 2, 65], BF16)
    V1PREV = sb("V1PREV", [128, 2, 2, 65], BF16)
    PTS = sb("PTS", [128, 3, 512], BF16)
    TMPA = sb("TMPA", [128, 2, 1024], BF16)
    TMPT = sb("TMPT", [128, 2, 512], F32)
    TMP2 = sb("TMP2", [128, 2, 512], F32)
    BIAS = sb("BIAS", [128, 16, 256], BF16)
    MSKB = sb("MSKB", [128, 256], BF16)
    BAND = sb("BAND", [128, 3, 4, 128], BF16)
    IDENT = sb("IDENT", [128, 128], F32)
    IDENTB = sb("IDENTB", [128, 128], BF16)
    I8 = sb("I8", [128, 128], BF16)
    ONES = sb("ONES", [128, 128], F32)
    G1 = sb("G1", [128, 2, 1024], F32)
    MOD = sb("MOD", [128, 2, 16], F32)
    BADA = sb("BADA", [128, 2, 16], F32)
    CSIL = sb("CSIL", [128, 8, 2], F32)
    CSILB = sb("CSILB", [128, 8, 2], BF16)
    PSC = sb("PSC", [128, 2, 8], F32)
    ESINK = sb("ESINK", [128, 2, 16], F32)
    VALID = sb("VALID", [128, 2], F32)
    DN = sb("DN", [128, 2, 16], F32)
    RN = sb("RN", [128, 2, 16], F32)
    ST = sb("ST", [128, 2, 2, 6], F32)
    MV = sb("MV", [128, 2, 2], F32)
    RSTD = sb("RSTD", [128, 2, 1], F32)
    NMR = sb("NMR", [128, 2, 1], F32)
    SD = sb("SD", [128, 2, 1], F32)
    EPSC = sb("EPSC", [128, 1], F32)
    WKK4 = sb("WKK4", [128, 8, 4, 128], BF16)
    WV = sb("WV", [128, 8, 128], BF16)
    WMIX = sb("WMIX", [128, 2, 4, 256], BF16)
    SL = sb("SL", [128, NSLAB, 8, 512], F32)

    QT = B0[:, :].rearrange("p (j t) -> p j t", j=8)
    AT = QT
    POOLED = QT
    LNT = B0[:, 0:4096].bitcast(F32).rearrange("p (a n) -> p a n", a=2)
    ATT = B1[:, :].rearrange("p (s n) -> p s n", s=5)
    PT = B1[:, :].rearrange("p (j t) -> p j t", j=8)
    T1 = B1[:, 0:4096].bitcast(F32).rearrange("p (a n) -> p a n", a=2)
    ZA = B2[:, :].rearrange("p (j t) -> p j t", j=8)
    BADAG = B2[:, 0:4096].bitcast(F32).rearrange("p (a n) -> p a n", a=2)
    GROW = TMPT[:, :, :].rearrange("p a n -> p (a n)")

    OLD_B0 = ([("QT", s) for s in range(6)] + [("AT", s) for s in range(6)] +
              [("POOLED", s, hb) for s in range(6) for hb in range(2)] + ["LNT"])
    OLD_B1 = [("ATT", s) for s in range(6)] + [("PT", s) for s in range(6)] + ["T1a", "T1b"]
    OLD_B2 = [("ZA", s, oc) for s in range(6) for oc in range(8)] + ["BADAG"]

    PS = nc.alloc_psum_tensor("ps", [128, 4096], F32)
    PSB = PS[:, :].bitcast(BF16)
    bctr = [0]

    def bank1():
        b = bctr[0] % 8
        bctr[0] += 1
        return b

    sctr = [0]

    def bank_sc():
        b = sctr[0] % 5
        sctr[0] += 1
        return b

    def bank2():
        if bctr[0] % 2:
            bctr[0] += 1
        b = bctr[0] % 8
        bctr[0] += 2
        return b

    def ps(b, a=0, n=512):
        return PS[:, b * 512 + a: b * 512 + a + n]

    def SLB(i):
        return SL[:, i].bitcast(BF16)

    evq = [0]

    def evac_eng():
        evq[0] += 1
        return "act" if evq[0] % 2 else "dve"

    def copy_op(eng, out, in_, reads, writes):
        if eng == "act":
            P.op("act", lambda e, o=out, i=in_: e.copy(o, i), reads=reads, writes=writes)
        elif eng == "dve":
            P.op("dve", lambda e, o=out, i=in_: e.tensor_copy(o, i), reads=reads, writes=writes)
        else:
            P.op("pool", lambda e, o=out, i=in_: e.tensor_copy(o, i), reads=reads, writes=writes)

    def mm(out, lhsT, rhs, start, stop, reads, bank):
        P.op("pe", lambda e, o=out, l=lhsT, r=rhs, s=start, t=stop: e.matmul(o, l, r, start=s, stop=t),
             reads=reads, writes=[("ps", bank)])

    def dma(eng, out, in_, key, reads, writes):
        P.op(eng, lambda e, o=out, i=in_: e.dma_start(out=o, in_=i), reads=reads, writes=writes, dma=(eng, key))

    def bcast(handle, off, n):
        return bass.AP(handle, off, [[0, 128], [1, n]])

    dma("sp", IDENT[:, :], ident_d, "c0", [], ["IDENT"])
    dma("sp", CSIL[:, :, :], c_lay, "c1", [], ["CSIL"])
    dma("sp", BADA[:, :, :], bada_fm, "c2", [], ["BADA"])
    dma("sp", VALID[:, :], blkvalid, "c3", [], ["VALID"])
    dma("sp", PSC[:, :, :], pscale_fm, "c4", [], ["PSC"])
    for l in range(2):
        dma("sp", ESINK[:, l, :], bcast(sinks_d, l * 16, 16), "c6", [], [("ESINK", l)])
    P.op("pool", lambda e: e.memset(ONES[:, :], 0.5), writes=["ONES"])
    P.op("pool", lambda e: e.memset(EPSC[:, :], EPS2), writes=["EPSC"])
    P.op("pool", lambda e: e.memset(WKK4[:, :, :, :], 0.0), writes=["WKK4"])
    P.op("act", lambda e: e.activation(CSILB[:, :, :], CSIL[:, :, :], AF.Silu), reads=["CSIL"], writes=["CSILB"])
    for l in range(2):
        P.op("act", lambda e, l=l: e.activation(ESINK[:, l, :], ESINK[:, l, :], AF.Exp),
             reads=[("ESINK", l)], writes=[("ESINK", l)])

    slab_specs = []
    st = dict(issued=0, used=0)

    def slab_issue_upto(k):
        while st["issued"] <= k and st["issued"] < len(slab_specs):
            i = st["issued"]
            eng, src, f32 = slab_specs[i]
            bi = i % NSLAB
            dst = SL[:, bi] if f32 else SLB(bi)
            dma(eng, dst, src, ("slab", bi), [], [("slab", bi)])
            st["issued"] += 1

    def acquire(n):
        k = st["used"]
        st["used"] += n
        slab_issue_upto(k + NSLAB - 1)
        return [(k + i) % NSLAB for i in range(n)]

    tiles = []
    for ti in range(4):
        for l in range(2):
            tiles.append(Tile(ti, l))

    def ada_spec(l, c):
        slab_specs.append(("pool", w_ada[l, :, c * 1024:(c + 1) * 1024].rearrange("(kc p) n -> p kc n", p=128), False))

    for c in range(3):
        ada_spec(0, c)

    def wslab(ap2d):
        return ("pool", ap2d.rearrange("(kc p) n -> p kc n", p=128), False)

    for ti_, T in enumerate(tiles):
        l = T.l
        slab_specs.append(wslab(w_in[l, :, 0:1024]))
        if ti_ == 0:
            ada_spec(1, 2)
            ada_spec(1, 0)
        slab_specs.append(wslab(w_in[l, :, 1280:2304]))
        if ti_ == 0:
            ada_spec(1, 1)
        slab_specs.append(wslab(w_ap[l]))
        slab_specs.append(wslab(w_in[l, :, 4352:5376]))
        slab_specs.append(wslab(w_in[l, :, 2304:3328]))
        slab_specs.append(wslab(w_in[l, :, 3328:4352]))
        slab_specs.append(wslab(w_pp[l]))
        slab_specs.append(wslab(w_in[l, :, 5376:6400]))
        slab_specs.append(wslab(w_out[l]))

    def load_x(T, slots):
        for s in slots:
            p = T.xp[s]
            dma("sp", X[:, p, :], x_in[T.o_of[s] + 2], ("x", p), [], [("X", p)])

    load_x(tiles[0], range(6))

    TK = [("TMPT", 0), ("TMPT", 1)]

    def ada_cols(l, c):
        (si,) = acquire(1)
        W = SLB(si)
        bmod = bank1()
        for oi in range(8):
            for kc in range(8):
                mm(ps(bmod, 2 * oi, 2), W[:, kc, oi * 128:(oi + 1) * 128], CSILB[:, kc, :],
                   kc == 0, kc == 7, [("slab", si), "CSILB"], bmod)
        P.op("dve", lambda e, l=l, c=c, b=bmod: e.tensor_tensor(
            MOD[:, l, c * 8:(c + 1) * 8], ps(b, 0, 16).rearrange("p (o t) -> p o t", t=2)[:, :, 0],
            BADA[:, l, c * 8:(c + 1) * 8], ALU.add),
            reads=[("ps", bmod), "BADA"], writes=[("MOD", l)])
        if c == 1:
            P.op("dve", lambda e, l=l: e.tensor_scalar(MOD[:, l, 8:16], MOD[:, l, 8:16], 1.0, None, ALU.add),
                 reads=[("MOD", l)], writes=[("MOD", l)])

    def ada_gate(l):
        P.alias(["BADAG"], OLD_B2)
        dma("sp", BADAG[:, l, :], bcast(bada_g, l * 1024, 1024), "c5", [], ["BADAG"])
        (si,) = acquire(1)
        W = SLB(si)
        for half in range(2):
            bg = bank1()
            for kc in range(8):
                mm(ps(bg, 0, 512)[0:2, :], CSILB[:, kc, :], W[:, kc, half * 512:(half + 1) * 512], kc == 0, kc == 7,
                   [("slab", si), "CSILB"], bg)
            P.op("dve", lambda e, l=l, b=bg, h=half: e.tensor_tensor(
                GROW[0:2, h * 512:(h + 1) * 512], ps(b, 0, 512)[0:2, :], BADAG[0:2, l, h * 512:(h + 1) * 512], ALU.add),
                reads=[("ps", bg), "BADAG"], writes=TK)
            P.op("dve", lambda e, h=half: e.tensor_scalar(
                GROW[0:2, h * 512:(h + 1) * 512], GROW[0:2, h * 512:(h + 1) * 512], 1.0, GSC, ALU.add, ALU.mult),
                reads=TK, writes=TK)
            bb = bank1()
            mm(ps(bb, 0, 512), ONES[0:2, :], GROW[0:2, half * 512:(half + 1) * 512], True, True,
               TK + ["ONES"], bb)
            copy_op("dve", G1[:, l, half * 512:(half + 1) * 512], ps(bb, 0, 512), [("ps", bb)], [("G1", l)])

    ada_cols(0, 0)
    ada_cols(0, 1)
    ada_gate(0)

    dma("pool", IDENTB[:, :], ident_d, "c7", [], ["IDENTB"])
    dma("pool", I8[:, :], i8_d, "c8", [], ["I8"])
    dma("pool", MSKB[:, :], msk, "c9", [], ["MSKB"])
    dma("pool", BIAS[:, :, :], rbg, "c10", [], ["BIAS"])
    for a in range(3):
        dma("pool", BAND[:, a, :, :], bands_d[a].rearrange("g s t -> s g t"), ("c11", a), [], ["BAND"])
    P.op("dve", lambda e: e.tensor_tensor(BIAS[:, :, :], BIAS[:, :, :],
                                          MSKB[:, :].unsqueeze(1).to_broadcast([128, 16, 256]), ALU.add),
         reads=["BIAS", "MSKB"], writes=["BIAS"])

    def phase_ut(T):
        l = T.l
        for grp in chunks(T.alls):
            n = len(grp) * 128
            for kc in range(8):
                b = bank1()
                for i, s in enumerate(grp):
                    P.op("pe", lambda e, b=b, i=i, s=s, kc=kc: e.transpose(
                        ps(b, i * 128, 128), X[:, T.xp[s], kc * 128:(kc + 1) * 128], IDENT[:, :]),
                        reads=[("X", T.xp[s]), "IDENT"], writes=[("ps", b)])
                out = UT[:, kc, grp[0] * 128: grp[0] * 128 + n]
                eng = evac_eng()
                rd = [("ps", b), ("MOD", l)]
                wr = [("UT", s) for s in grp]
                if eng == "act":
                    P.op("act", lambda e, o=out, b=b, n=n, l=l, kc=kc: e.activation(
                        o, ps(b, 0, n), AF.Identity, bias=MOD[:, l, kc:kc + 1], scale=MOD[:, l, 8 + kc:9 + kc]),
                        reads=rd, writes=wr)
                else:
                    P.op("dve", lambda e, o=out, b=b, n=n, l=l, kc=kc: e.tensor_scalar(
                        o, ps(b, 0, n), MOD[:, l, 8 + kc:9 + kc], MOD[:, l, kc:kc + 1], ALU.mult, ALU.add),
                        reads=rd, writes=wr)

    def ut_cols(grp):
        return grp[0] * 128, len(grp) * 128

    def phase_q(T):
        (si,) = acquire(1)
        W = SLB(si)
        P.alias([("QT", s) for s in T.full], OLD_B0)
        for grp in chunks(T.full):
            u0, n = ut_cols(grp)
            c0 = T.fcol(grp[0])
            for j in range(8):
                b = bank1()
                for kc in range(8):
                    mm(ps(b, 0, n), W[:, kc, j * 128:(j + 1) * 128], UT[:, kc, u0:u0 + n], kc == 0, kc == 7,
                       [("slab", si)] + [("UT", s) for s in grp], b)
                copy_op(evac_eng(), QT[:, j, c0:c0 + n], ps(b, 0, n), [("ps", b)], [("QT", s) for s in grp])

    def load_small(T):
        l = T.l
        for kvh in range(2):
            for e_ in range(2):
                v = kvh * 2 + e_
                src = w_in[l, :, 1024 + kvh * 64: 1024 + kvh * 64 + 64].rearrange("(kc p) n -> p kc n", p=128)
                dma("pool", WKK4[:, :, v, e_ * 64:(e_ + 1) * 64], src, ("wkk", v), [], ["WKK4"])
        dma("pool", WV[:, :, :], w_in[l, :, 1152:1280].rearrange("(kc p) n -> p kc n", p=128), "wv", [], ["WV"])
        for g in range(4):
            dma("pool", WMIX[:, :, g, :], w_mix[l, g].rearrange("(kc p) d -> p kc d", p=128), ("wmix", g), [], ["WMIX"])

    def phase_kv_attn(T):
        l = T.l
        for grp in chunks(T.alls):
            u0, n = ut_cols(grp)
            for v in range(4):
                b = bank1()
                for kc in range(8):
                    mm(ps(b, 0, n), WKK4[:, kc, v, :], UT[:, kc, u0:u0 + n], kc == 0, kc == 7,
                       ["WKK4"] + [("UT", s) for s in grp], b)
                copy_op(evac_eng(), KT4[:, v, u0:u0 + n], ps(b, 0, n), [("ps", b)], [("KT4", s) for s in grp])
        for s in T.alls:
            b = bank1()
            for kc in range(8):
                mm(ps(b, 0, 128), UT[:, kc, s * 128:(s + 1) * 128], WV[:, kc, :], kc == 0, kc == 7,
                   ["WV", ("UT", s)], b)
            src = ps(b, 0, 128).rearrange("p (h d) -> p h d", h=2)
            o = T.o_of[s]
            if o < 0:
                P.op("dve", lambda e, s=s, src=src, o=o: e.tensor_scalar(
                    V1[:, s, :, 0:64], src, VALID[:, o + 2:o + 3], None, ALU.mult),
                    reads=[("ps", b), "VALID"], writes=[("V1", s)])
                for h in range(2):
                    P.op("pool", lambda e, s=s, h=h, o=o: e.tensor_copy(V1[:, s, h, 64:65], VALID[:, o + 2:o + 3]),
                         reads=["VALID"], writes=[("V1", s)])
            else:
                copy_op(evac_eng(), V1[:, s, :, 0:64], src, [("ps", b)], [("V1", s)])
                P.op("pool", lambda e, s=s: e.memset(V1[:, s, :, 64:65], 1.0), writes=[("V1", s)])
        P.alias([("ATT", s) for s in T.full], OLD_B1)
        items = []
        for qs in T.full:
            for p in range(8):
                items.append((qs, p))
        pvbank = {}
        ring = [0]

        def emit_qk(qs, p):
            fs = qs - T.base
            kvh = p // 4
            b = bank_sc()
            r = ring[0] % 3
            ring[0] += 1
            inprev = (qs - 1) in T.alls
            for e_ in range(2):
                v = kvh * 2 + e_
                if inprev:
                    kprev = KT4[:, v, (qs - 1) * 128: qs * 128]
                    rprev = ("KT4", qs - 1)
                else:
                    kprev = KTPREV[:, l, v, :]
                    rprev = ("KTPREV", l)
                kown = KT4[:, v, qs * 128:(qs + 1) * 128]
                q = QT[:, p, fs * 128:(fs + 1) * 128]
                mm(ps(b, e_ * 256, 128), kprev, q, True, False, [rprev, ("QT", qs)], b)
                mm(ps(b, e_ * 256 + 128, 128), kown, q, False, False, [("KT4", qs), ("QT", qs)], b)
                mm(ps(b, e_ * 256, 256), I8[:, :], BIAS[:, 2 * p + e_, :], False, True, ["I8", "BIAS"], b)
            P.op("act", lambda e, b=b, r=r: e.activation(PTS[:, r, :], ps(b, 0, 512), AF.Exp, scale=0.125),
                 reads=[("ps", b)], writes=[("PTS", r)])
            return r

        def pv_loc(h):
            return (h // 7, (h % 7) * 65)

        def emit_pv(qs, p, r):
            kvh = p // 4
            inprev = (qs - 1) in T.alls
            if p == 0:
                pvbank[qs] = [5, 6, 7]
            for e_ in range(2):
                h = 2 * p + e_
                bi, off = pv_loc(h)
                b = pvbank[qs][bi]
                if inprev:
                    vprev = V1[:, qs - 1, kvh, :]
                    rprev = ("V1", qs - 1)
                else:
                    vprev = V1PREV[:, l, kvh, :]
                    rprev = ("V1PREV", l)
                mm(ps(b, off, 65), PTS[:, r, e_ * 256: e_ * 256 + 128], vprev, True, False, [("PTS", r), rprev], b)
                mm(ps(b, off, 65), PTS[:, r, e_ * 256 + 128: e_ * 256 + 256], V1[:, qs, kvh, :], False, True,
                   [("PTS", r), ("V1", qs)], b)
            if p == 7:
                fs = qs - T.base
                par = fs % 2
                for bi, (h0, h1) in enumerate([(0, 7), (7, 14), (14, 16)]):
                    b = pvbank[qs][bi]
                    nh = h1 - h0
                    pv = ps(b, 0, nh * 65).rearrange("p (h d) -> p h d", d=65)
                    P.op("dve", lambda e, pv=pv, h0=h0, h1=h1, par=par: e.tensor_tensor(
                        DN[:, par, h0:h1], pv[:, :, 64], ESINK[:, l, h0:h1], ALU.add),
                        reads=[("ps", b), ("ESINK", l)], writes=[("DN", par, bi)])
                    P.op("dve", lambda e, h0=h0, h1=h1, par=par: e.reciprocal(RN[:, par, h0:h1], DN[:, par, h0:h1]),
                         reads=[("DN", par, bi)], writes=[("RN", par, bi)])
                    P.op("dve", lambda e, pv=pv, h0=h0, h1=h1, nh=nh, par=par, fs=fs: e.tensor_tensor(
                        ATT[:, fs, h0 * 64:h1 * 64].rearrange("p (h d) -> p h d", d=64), pv[:, :, 0:64],
                        RN[:, par, h0:h1].unsqueeze(2).to_broadcast([128, nh, 64]), ALU.mult),
                        reads=[("ps", b), ("RN", par, bi)], writes=[("ATT", qs)])

        prev = None
        for (qs, p) in items:
            r = emit_qk(qs, p)
            if prev is not None:
                emit_pv(*prev)
            prev = (qs, p, r)
        emit_pv(*prev)
        last = T.full[-1]
        copy_op("pool", KTPREV[:, l, :, :], KT4[:, :, last * 128:(last + 1) * 128], [("KT4", last)], [("KTPREV", l)])
        copy_op("pool", V1PREV[:, l, :, :], V1[:, last, :, :], [("V1", last)], [("V1PREV", l)])

    def phase_agate(T):
        (si,) = acquire(1)
        W = SLB(si)
        for k, s in enumerate(T.full):
            fs = s - T.base
            b = bank2()
            for half in range(2):
                for kc in range(8):
                    mm(ps(b + half, 0, 512), UT[:, kc, s * 128:(s + 1) * 128], W[:, kc, half * 512:(half + 1) * 512],
                       kc == 0, kc == 7, [("slab", si), ("UT", s)], b + half)
            t = k % 2
            P.op("act", lambda e, b=b, t=t: e.activation(TMPA[:, t, :], PS[:, b * 512:(b + 2) * 512], AF.Silu),
                 reads=[("ps", b), ("ps", b + 1)], writes=[("TMPA", t)])
            P.op("pool", lambda e, fs=fs, t=t: e.tensor_tensor(ATT[:, fs, :], ATT[:, fs, :], TMPA[:, t, :], ALU.mult),
                 reads=[("ATT", s), ("TMPA", t)], writes=[("ATT", s)])

    def gated_proj(T, W1, s1, W2, s2, src, srckey, dst, dstkey, add_to=None, hook=None):
        k = 0
        for grp in chunks(T.full):
            u0, n = ut_cols(grp)
            c0 = T.fcol(grp[0])
            for oc in range(8):
                bm = bank1()
                for kc in range(8):
                    mm(ps(bm, 0, n), W2[:, kc, oc * 128:(oc + 1) * 128], UT[:, kc, u0:u0 + n], kc == 0, kc == 7,
                       [("slab", s2)] + [("UT", s) for s in grp], bm)
                ba = bank1()
                for kc in range(8):
                    mm(ps(ba, 0, n), W1[:, kc, oc * 128:(oc + 1) * 128], src[:, kc, c0:c0 + n], kc == 0, kc == 7,
                       [("slab", s1)] + [(srckey, s) for s in grp], ba)
                t = k % 2
                k += 1
                P.op("act", lambda e, bm=bm, n=n, t=t: e.activation(TMPT[:, t, 0:n], ps(bm, 0, n), AF.Tanh, scale=0.5),
                     reads=[("ps", bm)], writes=[("TMPT", t)])
                if add_to is None:
                    P.op("dve", lambda e, ba=ba, n=n, t=t, oc=oc, c0=c0: e.scalar_tensor_tensor(
                        dst[:, oc, c0:c0 + n], TMPT[:, t, 0:n], 1.0, ps(ba, 0, n), ALU.add, ALU.mult),
                        reads=[("ps", ba), ("TMPT", t)], writes=[(dstkey, s, oc) for s in grp])
                else:
                    P.op("dve", lambda e, ba=ba, n=n, t=t: e.scalar_tensor_tensor(
                        TMP2[:, t, 0:n], TMPT[:, t, 0:n], 1.0, ps(ba, 0, n), ALU.add, ALU.mult),
                        reads=[("ps", ba), ("TMPT", t)], writes=[("TMP2", t)])
                    P.op("pool", lambda e, n=n, t=t, oc=oc, c0=c0: e.tensor_tensor(
                        dst[:, oc, c0:c0 + n], add_to[:, oc, c0:c0 + n], TMP2[:, t, 0:n], ALU.add),
                        reads=[("TMP2", t)] + [(dstkey, s, oc) for s in grp], writes=[(dstkey, s, oc) for s in grp])
                if hook is not None and oc == 3 and grp[-1] == T.full[-1]:
                    hook()

    def phase_aproj(T):
        s1, s2 = acquire(2)
        P.alias([("AT", s) for s in T.full], OLD_B0)
        for s in T.full:
            fs = s - T.base
            b = bank1()
            for kc in range(8):
                P.op("pe", lambda e, b=b, kc=kc, fs=fs: e.transpose(
                    PSB[:, b * 1024 + kc * 128: b * 1024 + (kc + 1) * 128], ATT[:, fs, kc * 128:(kc + 1) * 128],
                    IDENTB[:, :]), reads=[("ATT", s), "IDENTB"], writes=[("ps", b)])
            P.op("dve", lambda e, b=b, fs=fs: e.tensor_copy(
                AT[:, :, fs * 128:(fs + 1) * 128], PSB[:, b * 1024:(b + 1) * 1024].rearrange("p (k t) -> p k t", k=8)),
                reads=[("ps", b)], writes=[("AT", s)])
        P.alias([("ZA", s, oc) for s in T.full for oc in range(8)], OLD_B2)
        gated_proj(T, SLB(s1), s1, SLB(s2), s2, AT, "AT", ZA, "ZA")

    def phase_pool(T):
        l = T.l
        s1, s2 = acquire(2)
        Wpin, Wpg = SLB(s1), SLB(s2)
        for k, s in enumerate(T.alls):
            b = bank2()
            for half in range(2):
                for kc in range(8):
                    mm(ps(b + half, 0, 512), UT[:, kc, s * 128:(s + 1) * 128], Wpin[:, kc, half * 512:(half + 1) * 512],
                       kc == 0, kc == 7, [("slab", s1), ("UT", s)], b + half)
            o = T.o_of[s]
            src = PS[:, b * 512:(b + 2) * 512]
            rd = [("ps", b), ("ps", b + 1)]
            if o < 0:
                P.op("dve", lambda e, s=s, src=src, o=o: e.tensor_scalar(
                    PIN[:, s, :], src, VALID[:, o + 2:o + 3], None, ALU.mult),
                    reads=rd + ["VALID"], writes=[("PIN", s)])
            else:
                copy_op(evac_eng(), PIN[:, s, :], src, rd, [("PIN", s)])
        P.alias([("POOLED", s, hb) for s in T.full for hb in range(2)], OLD_B0)
        for s in T.full:
            fs = s - T.base
            if (s - 1) in T.alls:
                prev = PIN[:, s - 1, :]
                rprev = ("PIN", s - 1)
            else:
                prev = PINPREV[:, l, :]
                rprev = ("PINPREV", l)
            own = 2 if T.o_of[s] == 0 else 1
            for hb in range(2):
                b = bank1()
                for q in range(4):
                    fc = hb * 4 + q
                    g = fc // 2
                    mm(ps(b, q * 128, 128), prev[:, fc * 128:(fc + 1) * 128], BAND[:, 0, g, :], True, False,
                       [rprev, "BAND"], b)
                    mm(ps(b, q * 128, 128), PIN[:, s, fc * 128:(fc + 1) * 128], BAND[:, own, g, :], False, True,
                       [("PIN", s), "BAND"], b)
                copy_op(evac_eng(), POOLED[:, hb * 4:hb * 4 + 4, fs * 128:(fs + 1) * 128],
                        ps(b, 0, 512).rearrange("p (q t) -> p q t", q=4), [("ps", b)], [("POOLED", s, hb)])
        last = T.full[-1]
        copy_op("pool", PINPREV[:, l, :], PIN[:, last, :], [("PIN", last)], [("PINPREV", l)])
        P.alias([("PT", s) for s in T.full], OLD_B1)
        k = 0
        for grp in chunks(T.full):
            u0, n = ut_cols(grp)
            c0 = T.fcol(grp[0])
            for oc in range(8):
                g = oc // 2
                bg = bank1()
                for kc in range(8):
                    mm(ps(bg, 0, n), Wpg[:, kc, oc * 128:(oc + 1) * 128], UT[:, kc, u0:u0 + n], kc == 0, kc == 7,
                       [("slab", s2)] + [("UT", s) for s in grp], bg)
                bm = bank1()
                for kc2 in range(2):
                    mm(ps(bm, 0, n), WMIX[:, kc2, g, (oc % 2) * 128:(oc % 2 + 1) * 128],
                       POOLED[:, 2 * g + kc2, c0:c0 + n], kc2 == 0, kc2 == 1,
                       ["WMIX"] + [("POOLED", s, (2 * g + kc2) // 4) for s in grp], bm)
                t = k % 2
                k += 1
                P.op("act", lambda e, bg=bg, n=n, t=t: e.activation(TMPA[:, t, 0:n], ps(bg, 0, n), AF.Silu),
                     reads=[("ps", bg)], writes=[("TMPA", t)])
                P.op("dve", lambda e, bm=bm, n=n, t=t, oc=oc, c0=c0: e.scalar_tensor_tensor(
                    PT[:, oc, c0:c0 + n], ps(bm, 0, n), PSC[:, l, oc:oc + 1], TMPA[:, t, 0:n], ALU.mult, ALU.mult),
                    reads=[("ps", bm), ("TMPA", t), "PSC"], writes=[("PT", s) for s in grp])

    def phase_pproj(T):
        l = T.l
        s1, s2 = acquire(2)
        P.alias(["LNT"], OLD_B0)
        dma("sp", LNT[:, 0, :], bcast(ln_g, l * 1024, 1024), "lnt", [], ["LNT"])
        dma("sp", LNT[:, 1, :], bcast(ln_b, l * 1024, 1024), "lnt", [], ["LNT"])
        so = st["used"] % NSLAB

        def scale_wout():
            Wo = SLB(so)
            P.op("pool", lambda e: e.tensor_tensor(
                Wo[:, :, :], Wo[:, :, :], G1[:, l, :].unsqueeze(1).to_broadcast([128, 8, 1024]), ALU.mult),
                reads=[("slab", so), ("G1", l)], writes=[("slab", so)])

        gated_proj(T, SLB(s1), s1, SLB(s2), s2, PT, "PT", ZA, "ZA", add_to=ZA)

    def phase_out(T, after_last_s1=None):
        l = T.l
        (si,) = acquire(1)
        W = SLB(si)
        P.alias(["T1a", "T1b"], OLD_B1)
        nb = len(T.full)

        def stage1(k):
            s = T.full[k]
            fs = s - T.base
            par = k % 2
            tk = "T1a" if par == 0 else "T1b"
            xs = T.xp[s]
            b = bank2()
            for half in range(2):
                for kc in range(8):
                    mm(ps(b + half, 0, 512), ZA[:, kc, fs * 128:(fs + 1) * 128], W[:, kc, half * 512:(half + 1) * 512],
                       kc == 0, kc == 7, [("slab", si)] + [("ZA", s, oc) for oc in range(8)], b + half)
            P.op("dve", lambda e, b=b, par=par: e.tensor_tensor(
                T1[:, par, :], PS[:, b * 512:(b + 2) * 512], G1[:, l, :], ALU.mult),
                reads=[("ps", b), ("ps", b + 1), ("G1", l)], writes=[tk])
            P.op("pool", lambda e, par=par, xs=xs: e.tensor_tensor(T1[:, par, :], T1[:, par, :], X[:, xs, :], ALU.add),
                 reads=[tk, ("X", xs)], writes=[tk])

        def stage2(k):
            s = T.full[k]
            par = k % 2
            tk = "T1a" if par == 0 else "T1b"
            xs = T.xp[s]
            for hh in range(2):
                P.op("dve", lambda e, par=par, hh=hh: e.bn_stats(ST[:, par, hh, :], T1[:, par, hh * 512:(hh + 1) * 512]),
                     reads=[tk], writes=[("ST", par, hh)])
            P.op("dve", lambda e, par=par: e.bn_aggr(MV[:, par, :], ST[:, par, :, :].rearrange("p a b -> p (a b)")),
                 reads=[("ST", par, 0), ("ST", par, 1)], writes=[("MV", par)])
            P.op("act", lambda e, par=par: e.activation(
                SD[:, par, :], MV[:, par, 1:2], AF.Sqrt, bias=EPSC[:, 0:1], scale=1.0),
                reads=[("MV", par), "EPSC"], writes=[("SD", par)])
            P.op("dve", lambda e, par=par: e.reciprocal(RSTD[:, par, :], SD[:, par, :]),
                 reads=[("SD", par)], writes=[("RSTD", par)])
            P.op("dve", lambda e, par=par: e.scalar_tensor_tensor(
                NMR[:, par, :], MV[:, par, 0:1], -1.0, RSTD[:, par, :], ALU.mult, ALU.mult),
                reads=[("MV", par), ("RSTD", par)], writes=[("NMR", par)])
            P.op("act", lambda e, par=par, xs=xs: e.activation(
                X[:, xs, :], T1[:, par, :], AF.Identity, bias=NMR[:, par, :], scale=RSTD[:, par, :]),
                reads=[tk, ("NMR", par), ("RSTD", par)], writes=[("X", xs)])

        def stage3(k):
            s = T.full[k]
            xs = T.xp[s]
            P.op("dve", lambda e, xs=xs: e.tensor_tensor(X[:, xs, :], X[:, xs, :], LNT[:, 0, :], ALU.mult),
                 reads=[("X", xs), "LNT"], writes=[("X", xs)])
            P.op("pool", lambda e, xs=xs: e.tensor_tensor(X[:, xs, :], X[:, xs, :], LNT[:, 1, :], ALU.add),
                 reads=[("X", xs), "LNT"], writes=[("X", xs)])
            if l == DEPTH - 1:
                o = T.o_of[s]
                dma("sp", out_d[o], X[:, xs, :], ("out", xs), [("X", xs)], [("OUT", o)])
                if T.idx < 3 and k < 2:
                    load_x(Tile(T.idx + 1, 0), [3 + k])

        for step in range(nb + 2):
            if step < nb:
                stage1(step)
            if 0 <= step - 1 < nb:
                stage2(step - 1)
            if 0 <= step - 2 < nb:
                stage3(step - 2)
            if step == nb - 1 and after_last_s1 is not None:
                after_last_s1()

    tapd = {}

    def tap(name, ap, shape, dt):
        t = nc.dram_tensor("tap_" + name, list(shape), dt, kind="ExternalOutput").ap()
        tapd[name] = t
        return t

    for ti, T in enumerate(tiles):
        if not (T.l == 0 and T.idx > 0):
            load_small(T)
            phase_ut(T)
        if T.l == 1 and T.idx < 3:
            load_x(Tile(T.idx + 1, 0), [1, 2])
        phase_q(T)
        if ti == 0:
            ada_gate(1)
        phase_kv_attn(T)
        if ti == 0:
            ada_cols(1, 0)
        phase_agate(T)
        if ti == 0:
            ada_cols(1, 1)
        phase_aproj(T)
        phase_pool(T)
        phase_pproj(T)
        if T.l == 1 and T.idx < 3:
            nT = tiles[ti + 1]

            def early(nT=nT):
                load_small(nT)
                phase_ut(nT)
            phase_out(T, after_last_s1=early)
        else:
            phase_out(T)
        if ("x1", ti) in taps:
            t = tap("x1_%d" % ti, None, [128, 6, 1024], F32)
            dma("sp", t, X[:, :, :], ("tap", ti), [("X", s) for s in range(6)], [("TAP", ti)])

    P.emit()
    return nc


def _t5_bucket(dist):
    max_exact = 16
    d = np.maximum(dist, 1).astype(np.float32)
    large = max_exact + (np.log(d / max_exact) / math.log(128 / max_exact) * (32 - max_exact)).astype(np.int32)
    large = np.minimum(large, 31)
    return np.where(dist < max_exact, dist, large)


def _structure():
    k = np.arange(128)[:, None]
    col = np.arange(256)[None, :]
    q = col % 128
    dist = np.where(col < 128, q + 128 - k, q - k)
    valid = (dist >= 0) & (dist < 128)
    bucket = _t5_bucket(np.maximum(dist, 0))
    msk = np.where(valid, 0.0, NEG).astype(np.float32)
    s = np.arange(128)[:, None]
    t = np.arange(128)[None, :]
    bands = np.zeros((3, 4, 128, 128), np.float32)
    for g, w in enumerate(POOL_WINDOWS):
        bands[0, g] = np.where(s - 128 >= t - w + 1, 1.0 / w, 0.0)
        own = np.where((s <= t) & (s >= t - w + 1), 1.0 / w, 0.0) - (s == t)
        bands[1, g] = own
        cnt = np.minimum(t + 1, w).astype(np.float32)
        bands[2, g] = np.where((s <= t) & (s >= t - w + 1), 1.0 / cnt, 0.0) - (s == t)
    return bucket, valid, msk, bands


_NC_CACHE = {}


def _host_inputs(inputs):
    x = np.asarray(inputs["x"], np.float32)
    c = np.asarray(inputs["c"], np.float32)
    rel_bias = np.asarray(inputs["rel_bias"], np.float32)
    b_ada = np.asarray(inputs["b_ada"], np.float32)
    pool_scale = np.asarray(inputs["pool_scale"], np.float32)
    bucket, valid, msk, bands = _structure()
    rbg = rel_bias[bucket]
    rbg = np.where(valid[:, :, None], rbg, np.float32(0.0))
    rbg = np.ascontiguousarray(np.transpose(rbg, (0, 2, 1)))
    ident = np.eye(128, dtype=np.float32)
    i8 = np.eye(128, dtype=np.float32) * 8.0
    bada_fm = np.ascontiguousarray(np.transpose(b_ada[:, :2048].reshape(2, 16, 128), (2, 0, 1)))
    bada_g = np.ascontiguousarray(b_ada[:, 2048:3072])
    pscale_fm = np.ascontiguousarray(np.transpose(pool_scale.reshape(2, 8, 128), (2, 0, 1)))
    shared = dict(
        w_ada=np.asarray(inputs["w_ada"], np.float32), bada_fm=bada_fm, bada_g=bada_g,
        w_in=np.asarray(inputs["w_in"], np.float32), w_ap=np.asarray(inputs["w_attn_proj"], np.float32),
        w_pp=np.asarray(inputs["w_pool_proj"], np.float32), w_out=np.asarray(inputs["w_out"], np.float32),
        w_mix=np.asarray(inputs["w_pool_mix"], np.float32), pscale_fm=pscale_fm,
        sinks=np.asarray(inputs["sinks"], np.float32), ln_g=np.asarray(inputs["ln_gain"], np.float32),
        ln_b=np.asarray(inputs["ln_bias"], np.float32), rbg=rbg, msk=msk, ident=ident, i8=i8,
    )
    in_maps = []
    for core in range(8):
        b, qd = core // 4, core % 4
        t0 = qd * 2048
        xin = np.zeros((18, 128, 1024), np.float32)
        if qd == 0:
            xin[2:] = x[b, 0:2048].reshape(16, 128, 1024)
        else:
            xin[:] = x[b, t0 - 256:t0 + 2048].reshape(18, 128, 1024)
        bv = np.full((128, 2), 0.0 if qd == 0 else 1.0, np.float32)
        bd = bands.copy()
        if qd != 0:
            bd[2] = bd[1]
        m = dict(shared)
        m.update(x_in=xin, c_lay=np.ascontiguousarray(np.repeat(c[b].reshape(8, 128).T[:, :, None], 2, axis=2)), blkvalid=bv, bands=bd)
        in_maps.append(m)
    return in_maps


def kernel(**inputs):
    in_maps = _host_inputs(inputs)
    if "nc" not in _NC_CACHE:
        _NC_CACHE["nc"] = build_program()
    nc = _NC_CACHE["nc"]
    res = run_bass_kernel_spmd(nc, in_maps, core_ids=list(range(8)))
    out = np.zeros((2, 8192, 1024), np.float32)
    for core in range(8):
        b, qd = core // 4, core % 4
        out[b, qd * 2048:(qd + 1) * 2048] = np.asarray(res.results[core]["out"]).reshape(2048, 1024)
    return out
```

```python
import math
import numpy as np
import concourse.bass as bass
import concourse.mybir as mybir
from concourse.bass_utils import run_bass_kernel_spmd

F32 = mybir.dt.float32
BF16 = mybir.dt.bfloat16
AF = mybir.ActivationFunctionType
ALU = mybir.AluOpType

D = 1024
DEPTH = 2
NOWN = 16
ALPHA = (2 * DEPTH) ** 0.25
LN_EPS = 1e-5
EPS2 = LN_EPS / (ALPHA * ALPHA)
GSC = 0.5 / ALPHA
NSLAB = 4
POOL_WINDOWS = (2, 4, 8, 16)
NEG = -1e30


class Prog:
    def __init__(self, nc):
        self.nc = nc
        self.ops = []
        self.last_w = {}
        self.readers = {}
        self.dma_last = {}

    def op(self, eng, fn, reads=(), writes=(), dma=None):
        idx = len(self.ops)
        me = ("dma", dma) if dma is not None else eng
        deps = set()
        for r in reads:
            w = self.last_w.get(r)
            if w is not None:
                deps.add((w, True))
        for r in writes:
            w = self.last_w.get(r)
            if w is not None:
                deps.add((w, False))
            for rd in self.readers.get(r, {}).values():
                deps.add((rd, False))
        if dma is not None and dma in self.dma_last:
            deps.add((self.dma_last[dma], True))
        final = set()
        for (d, raw) in deps:
            de = self.ops[d]["me"]
            if de == me and not isinstance(me, tuple):
                if raw and me != "pe":
                    final.add(d)
            else:
                final.add(d)
        for r in reads:
            self.readers.setdefault(r, {})[me] = idx
        for r in writes:
            self.last_w[r] = idx
            self.readers[r] = {}
        if dma is not None:
            self.dma_last[dma] = idx
        self.ops.append(dict(eng=eng, me=me, fn=fn, deps=final, dma=dma, inc=False))
        return idx

    def alias(self, new_keys, old_keys):
        acc = {}
        for o in old_keys:
            if o in self.last_w:
                w = self.last_w[o]
                k = self.ops[w]["me"]
                acc[k] = max(acc.get(k, -1), w)
            for k, rd in self.readers.get(o, {}).items():
                acc[k] = max(acc.get(k, -1), rd)
        for n in new_keys:
            cur = self.readers.setdefault(n, {})
            for k, v in acc.items():
                cur[k] = max(cur.get(k, -1), v)

    def emit(self):
        nc = self.nc
        ops = self.ops
        for o in ops:
            for d in o["deps"]:
                ops[d]["inc"] = True
        engs = ["pe", "act", "dve", "pool", "sp"]
        esem = {e: nc.alloc_semaphore("sem_" + e) for e in engs}
        dkeys = []
        for o in ops:
            if o["dma"] is not None and o["dma"] not in dkeys:
                dkeys.append(o["dma"])
        dsem = {k: nc.alloc_semaphore("dsem_%d" % i) for i, k in enumerate(dkeys)}
        cnt = {e: 0 for e in engs}
        dcnt = {k: 0 for k in dkeys}
        for o in ops:
            if o["dma"] is not None:
                dcnt[o["dma"]] += 16
                o["sem"] = dsem[o["dma"]]
                o["val"] = dcnt[o["dma"]]
            elif o["inc"]:
                cnt[o["eng"]] += 1
                o["sem"] = esem[o["eng"]]
                o["val"] = cnt[o["eng"]]
        final_dma = dict(dcnt)

        def run(engname):
            def body(e):
                waited = {}
                for o in ops:
                    if o["eng"] != engname:
                        continue
                    for d in sorted(o["deps"]):
                        dd = ops[d]
                        s, v = dd["sem"], dd["val"]
                        if waited.get(s.num if hasattr(s, "num") else id(s), 0) < v:
                            e.wait_ge(s, v)
                            waited[s.num if hasattr(s, "num") else id(s)] = v
                    ins = o["fn"](e)
                    if o["dma"] is not None:
                        ins.then_inc(o["sem"], 16)
                    elif o["inc"]:
                        ins.then_inc(o["sem"], 1)
                if engname == "sp":
                    for k in dkeys:
                        if final_dma[k] > 0:
                            e.wait_ge(dsem[k], final_dma[k])
            return body

        with nc.Block() as block:
            block.tensor(run("pe"))
            block.scalar(run("act"))
            block.vector(run("dve"))
            block.gpsimd(run("pool"))
            block.sync(run("sp"))


def chunks(slots):
    slots = list(slots)
    r = len(slots) % 4
    out = []
    if r:
        out.append(slots[:r])
    for i in range(r, len(slots), 4):
        out.append(slots[i:i + 4])
    return out


class Tile:
    def __init__(self, idx, l):
        self.idx = idx
        self.l = l
        if idx == 0:
            if l == 0:
                self.halo, self.full = 0, [1, 2, 3, 4, 5]
            else:
                self.halo, self.full = 1, [2, 3, 4, 5]
            self.o_of = {s: s - 2 for s in range(6)}
        else:
            self.halo, self.full = None, [1, 2, 3, 4]
            self.o_of = {s: 4 * idx + (s - 1) for s in range(1, 5)}
        if idx == 0:
            self.xp = {s: s for s in range(6)}
        else:
            ring = {1: [0, 1, 2, 3], 2: [4, 5, 0, 1], 3: [2, 3, 4, 5]}[idx]
            self.xp = {s: ring[s - 1] for s in range(1, 5)}
        self.base = self.full[0]
        self.alls = ([self.halo] if self.halo is not None else []) + self.full

    def fcol(self, s):
        return (s - self.base) * 128


def build_program(taps=()):
    nc = bass.Bass("TRN2", target_bir_lowering=False)
    P = Prog(nc)

    def din(name, shape, dt=F32):
        return nc.dram_tensor(name, list(shape), dt, kind="ExternalInput")

    x_in = din("x_in", [18, 128, 1024]).ap()
    c_lay = din("c_lay", [128, 8, 2]).ap()
    w_ada = din("w_ada", [2, 1024, 3072]).ap()
    bada_fm = din("bada_fm", [128, 2, 16]).ap()
    bada_g = din("bada_g", [2, 1024])
    w_in = din("w_in", [2, 1024, 6400]).ap()
    w_ap = din("w_ap", [2, 1024, 1024]).ap()
    w_pp = din("w_pp", [2, 1024, 1024]).ap()
    w_out = din("w_out", [2, 1024, 1024]).ap()
    w_mix = din("w_mix", [2, 4, 256, 256]).ap()
    pscale_fm = din("pscale_fm", [128, 2, 8]).ap()
    sinks_d = din("sinks", [2, 16])
    ln_g = din("ln_g", [2, 1024])
    ln_b = din("ln_b", [2, 1024])
    rbg = din("rbg", [128, 16, 256]).ap()
    msk = din("msk", [128, 256]).ap()
    bands_d = din("bands", [3, 4, 128, 128]).ap()
    blkvalid = din("blkvalid", [128, 2]).ap()
    ident_d = din("ident", [128, 128]).ap()
    i8_d = din("i8", [128, 128]).ap()
    out_d = nc.dram_tensor("out", [16, 128, 1024], F32, kind="ExternalOutput").ap()

    sb = nc.alloc_sbuf_tensor
    X = sb("X", [128, 6, 1024], F32)
    UT = sb("UT", [128, 8, 768], BF16)
    B0 = sb("B0", [128, 5120], BF16)
    B1 = sb("B1", [128, 5120], BF16)
    B2 = sb("B2", [128, 5120], BF16)
    PIN = sb("PIN", [128, 6, 1024], BF16)
    PINPREV = sb("PINPREV", [128, 2, 1024], BF16)
    KT4 = sb("KT4", [128, 4, 768], BF16)
    KTPREV = sb("KTPREV", [128, 2, 4, 128], BF16)
    V1 = sb("V1", [128, 6, 2, 65], BF16)
    V1PREV = sb("V1PREV", [128, 2, 2, 65], BF16)
    PTS = sb("PTS", [128, 3, 512], BF16)
    TMPA = sb("TMPA", [128, 2, 1024], BF16)
    TMPT = sb("TMPT", [128, 2, 512], F32)
    TMP2 = sb("TMP2", [128, 2, 512], F32)
    BIAS = sb("BIAS", [128, 16, 256], BF16)
    MSKB = sb("MSKB", [128, 256], BF16)
    BAND = sb("BAND", [128, 3, 4, 128], BF16)
    IDENT = sb("IDENT", [128, 128], F32)
    IDENTB = sb("IDENTB", [128, 128], BF16)
    I8 = sb("I8", [128, 128], BF16)
    ONES = sb("ONES", [128, 128], F32)
    G1 = sb("G1", [128, 2, 1024], F32)
    MOD = sb("MOD", [128, 2, 16], F32)
    BADA = sb("BADA", [128, 2, 16], F32)
    CSIL = sb("CSIL", [128, 8, 2], F32)
    CSILB = sb("CSILB", [128, 8, 2], BF16)
    PSC = sb("PSC", [128, 2, 8], F32)
    ESINK = sb("ESINK", [128, 2, 16], F32)
    VALID = sb("VALID", [128, 2], F32)
    DN = sb("DN", [128, 2, 16], F32)
    RN = sb("RN", [128, 2, 16], F32)
    ST = sb("ST", [128, 2, 2, 6], F32)
    MV = sb("MV", [128, 2, 2], F32)
    RSTD = sb("RSTD", [128, 2, 1], F32)
    NMR = sb("NMR", [128, 2, 1], F32)
    SD = sb("SD", [128, 2, 1], F32)
    EPSC = sb("EPSC", [128, 1], F32)
    WKK4 = sb("WKK4", [128, 8, 4, 128], BF16)
    WV = sb("WV", [128, 8, 128], BF16)
    WMIX = sb("WMIX", [128, 2, 4, 256], BF16)
    SL = sb("SL", [128, NSLAB, 8, 512], F32)

    QT = B0[:, :].rearrange("p (j t) -> p j t", j=8)
    AT = QT
    POOLED = QT
    LNT = B0[:, 0:4096].bitcast(F32).rearrange("p (a n) -> p a n", a=2)
    ATT = B1[:, :].rearrange("p (s n) -> p s n", s=5)
    PT = B1[:, :].rearrange("p (j t) -> p j t", j=8)
    T1 = B1[:, 0:4096].bitcast(F32).rearrange("p (a n) -> p a n", a=2)
    ZA = B2[:, :].rearrange("p (j t) -> p j t", j=8)
    BADAG = B2[:, 0:4096].bitcast(F32).rearrange("p (a n) -> p a n", a=2)
    GROW = TMPT[:, :, :].rearrange("p a n -> p (a n)")

    OLD_B0 = ([("QT", s) for s in range(6)] + [("AT", s) for s in range(6)] +
              [("POOLED", s, hb) for s in range(6) for hb in range(2)] + ["LNT"])
    OLD_B1 = [("ATT", s) for s in range(6)] + [("PT", s) for s in range(6)] + ["T1a", "T1b"]
    OLD_B2 = [("ZA", s, oc) for s in range(6) for oc in range(8)] + ["BADAG"]

    STG = PIN[:, 0:4, :].rearrange("p s n -> p (s n)").bitcast(F32).rearrange("p (a n) -> p a n", a=2)
    OLD_PIN = [("PIN", s) for s in range(6)] + ["STG0", "STG1"]

    PS = nc.alloc_psum_tensor("ps", [128, 4096], F32)
    PSB = PS[:, :].bitcast(BF16)
    bctr = [0]

    def bank1():
        b = bctr[0] % 8
        bctr[0] += 1
        return b

    sctr = [0]

    def bank_sc():
        b = sctr[0] % 5
        sctr[0] += 1
        return b

    def bank2():
        if bctr[0] % 2:
            bctr[0] += 1
        b = bctr[0] % 8
        bctr[0] += 2
        return b

    def ps(b, a=0, n=512):
        return PS[:, b * 512 + a: b * 512 + a + n]

    def SLB(i):
        return SL[:, i].bitcast(BF16)

    evq = [0]

    def evac_eng():
        evq[0] += 1
        return "act" if evq[0] % 2 else "dve"

    def copy_op(eng, out, in_, reads, writes):
        if eng == "act":
            P.op("act", lambda e, o=out, i=in_: e.copy(o, i), reads=reads, writes=writes)
        elif eng == "dve":
            P.op("dve", lambda e, o=out, i=in_: e.tensor_copy(o, i), reads=reads, writes=writes)
        else:
            P.op("pool", lambda e, o=out, i=in_: e.tensor_copy(o, i), reads=reads, writes=writes)

    def mm(out, lhsT, rhs, start, stop, reads, bank):
        P.op("pe", lambda e, o=out, l=lhsT, r=rhs, s=start, t=stop: e.matmul(o, l, r, start=s, stop=t),
             reads=reads, writes=[("ps", bank)])

    def dma(eng, out, in_, key, reads, writes):
        P.op(eng, lambda e, o=out, i=in_: e.dma_start(out=o, in_=i), reads=reads, writes=writes, dma=(eng, key))

    def bcast(handle, off, n):
        return bass.AP(handle, off, [[0, 128], [1, n]])

    dma("sp", IDENT[:, :], ident_d, "c0", [], ["IDENT"])
    dma("sp", CSIL[:, :, :], c_lay, "c1", [], ["CSIL"])
    dma("sp", BADA[:, :, :], bada_fm, "c2", [], ["BADA"])
    dma("sp", VALID[:, :], blkvalid, "c3", [], ["VALID"])
    dma("sp", PSC[:, :, :], pscale_fm, "c4", [], ["PSC"])
    for l in range(2):
        dma("sp", ESINK[:, l, :], bcast(sinks_d, l * 16, 16), "c6", [], [("ESINK", l)])
    dma("pool", IDENTB[:, :], ident_d, "c7", [], ["IDENTB"])
    dma("pool", I8[:, :], i8_d, "c8", [], ["I8"])
    dma("pool", MSKB[:, :], msk, "c9", [], ["MSKB"])
    dma("pool", BIAS[:, :, :], rbg, "c10", [], ["BIAS"])
    for a in range(3):
        dma("pool", BAND[:, a, :, :], bands_d[a].rearrange("g s t -> s g t"), ("c11", a), [], ["BAND"])
    P.op("pool", lambda e: e.memset(ONES[:, :], 0.5), writes=["ONES"])
    P.op("pool", lambda e: e.memset(EPSC[:, :], EPS2), writes=["EPSC"])
    P.op("pool", lambda e: e.memset(WKK4[:, :, :, :], 0.0), writes=["WKK4"])
    P.op("dve", lambda e: e.tensor_tensor(BIAS[:, :, :], BIAS[:, :, :],
                                          MSKB[:, :].unsqueeze(1).to_broadcast([128, 16, 256]), ALU.add),
         reads=["BIAS", "MSKB"], writes=["BIAS"])
    P.op("act", lambda e: e.activation(CSILB[:, :, :], CSIL[:, :, :], AF.Silu), reads=["CSIL"], writes=["CSILB"])
    for l in range(2):
        P.op("act", lambda e, l=l: e.activation(ESINK[:, l, :], ESINK[:, l, :], AF.Exp),
             reads=[("ESINK", l)], writes=[("ESINK", l)])

    slab_specs = []
    st = dict(issued=0, used=0)

    def slab_issue_upto(k):
        while st["issued"] <= k and st["issued"] < len(slab_specs):
            i = st["issued"]
            eng, src, f32 = slab_specs[i]
            bi = i % NSLAB
            dst = SL[:, bi] if f32 else SLB(bi)
            dma(eng, dst, src, ("slab", bi), [], [("slab", bi)])
            st["issued"] += 1

    def acquire(n):
        k = st["used"]
        st["used"] += n
        slab_issue_upto(k + NSLAB - 1)
        return [(k + i) % NSLAB for i in range(n)]

    tiles = []
    for ti in range(4):
        for l in range(2):
            tiles.append(Tile(ti, l))

    def ada_spec(l, c):
        slab_specs.append(("pool", w_ada[l, :, c * 1024:(c + 1) * 1024].rearrange("(kc p) n -> p kc n", p=128), False))

    for c in range(2):
        ada_spec(0, c)

    def wslab(ap2d):
        return ("pool", ap2d.rearrange("(kc p) n -> p kc n", p=128), False)

    for ti_, T in enumerate(tiles):
        l = T.l
        slab_specs.append(wslab(w_in[l, :, 0:1024]))
        if ti_ == 0:
            ada_spec(0, 2)
            ada_spec(1, 2)
            ada_spec(1, 0)
        slab_specs.append(wslab(w_in[l, :, 1280:2304]))
        if ti_ == 0:
            ada_spec(1, 1)
        slab_specs.append(wslab(w_ap[l]))
        slab_specs.append(wslab(w_in[l, :, 4352:5376]))
        slab_specs.append(wslab(w_in[l, :, 2304:3328]))
        slab_specs.append(wslab(w_in[l, :, 3328:4352]))
        slab_specs.append(wslab(w_pp[l]))
        slab_specs.append(wslab(w_in[l, :, 5376:6400]))
        slab_specs.append(wslab(w_out[l]))

    def load_x(T, slots):
        for s in slots:
            p = T.xp[s]
            dma("sp", X[:, p, :], x_in[T.o_of[s] + 2], ("x", p), [], [("X", p)])

    load_x(tiles[0], range(6))

    TK = [("TMPT", 0), ("TMPT", 1)]

    def ada_cols(l, c):
        (si,) = acquire(1)
        W = SLB(si)
        bmod = bank1()
        for oi in range(8):
            for kc in range(8):
                mm(ps(bmod, 2 * oi, 2), W[:, kc, oi * 128:(oi + 1) * 128], CSILB[:, kc, :],
                   kc == 0, kc == 7, [("slab", si), "CSILB"], bmod)
        P.op("dve", lambda e, l=l, c=c, b=bmod: e.tensor_tensor(
            MOD[:, l, c * 8:(c + 1) * 8], ps(b, 0, 16).rearrange("p (o t) -> p o t", t=2)[:, :, 0],
            BADA[:, l, c * 8:(c + 1) * 8], ALU.add),
            reads=[("ps", bmod), "BADA"], writes=[("MOD", l)])
        if c == 1:
            P.op("dve", lambda e, l=l: e.tensor_scalar(MOD[:, l, 8:16], MOD[:, l, 8:16], 1.0, None, ALU.add),
                 reads=[("MOD", l)], writes=[("MOD", l)])

    def ada_gate(l):
        P.alias(["BADAG"], OLD_B2)
        dma("sp", BADAG[:, l, :], bcast(bada_g, l * 1024, 1024), "c5", [], ["BADAG"])
        (si,) = acquire(1)
        W = SLB(si)
        for half in range(2):
            bg = bank1()
            for kc in range(8):
                mm(ps(bg, 0, 512)[0:2, :], CSILB[:, kc, :], W[:, kc, half * 512:(half + 1) * 512], kc == 0, kc == 7,
                   [("slab", si), "CSILB"], bg)
            P.op("dve", lambda e, l=l, b=bg, h=half: e.tensor_tensor(
                GROW[0:2, h * 512:(h + 1) * 512], ps(b, 0, 512)[0:2, :], BADAG[0:2, l, h * 512:(h + 1) * 512], ALU.add),
                reads=[("ps", bg), "BADAG"], writes=TK)
            P.op("dve", lambda e, h=half: e.tensor_scalar(
                GROW[0:2, h * 512:(h + 1) * 512], GROW[0:2, h * 512:(h + 1) * 512], 1.0, GSC, ALU.add, ALU.mult),
                reads=TK, writes=TK)
            bb = bank1()
            mm(ps(bb, 0, 512), ONES[0:2, :], GROW[0:2, half * 512:(half + 1) * 512], True, True,
               TK + ["ONES"], bb)
            copy_op("dve", G1[:, l, half * 512:(half + 1) * 512], ps(bb, 0, 512), [("ps", bb)], [("G1", l)])

    ada_cols(0, 0)
    ada_cols(0, 1)

    def phase_ut(T):
        l = T.l
        for grp in chunks(T.alls):
            n = len(grp) * 128
            for kc in range(8):
                b = bank1()
                for i, s in enumerate(grp):
                    if T.l == 0 and T.idx > 0 and s >= 3:
                        xsrc, xkey = STG[:, s - 3, :], "STG%d" % (s - 3)
                    else:
                        xsrc, xkey = X[:, T.xp[s], :], ("X", T.xp[s])
                    P.op("pe", lambda e, b=b, i=i, kc=kc, xsrc=xsrc: e.transpose(
                        ps(b, i * 128, 128), xsrc[:, kc * 128:(kc + 1) * 128], IDENT[:, :]),
                        reads=[xkey, "IDENT"], writes=[("ps", b)])
                out = UT[:, kc, grp[0] * 128: grp[0] * 128 + n]
                eng = evac_eng()
                rd = [("ps", b), ("MOD", l)]
                wr = [("UT", s) for s in grp]
                if eng == "act":
                    P.op("act", lambda e, o=out, b=b, n=n, l=l, kc=kc: e.activation(
                        o, ps(b, 0, n), AF.Identity, bias=MOD[:, l, kc:kc + 1], scale=MOD[:, l, 8 + kc:9 + kc]),
                        reads=rd, writes=wr)
                else:
                    P.op("dve", lambda e, o=out, b=b, n=n, l=l, kc=kc: e.tensor_scalar(
                        o, ps(b, 0, n), MOD[:, l, 8 + kc:9 + kc], MOD[:, l, kc:kc + 1], ALU.mult, ALU.add),
                        reads=rd, writes=wr)

    def ut_cols(grp):
        return grp[0] * 128, len(grp) * 128

    def phase_q(T):
        (si,) = acquire(1)
        W = SLB(si)
        P.alias([("QT", s) for s in T.full], OLD_B0)
        for grp in chunks(T.full):
            u0, n = ut_cols(grp)
            c0 = T.fcol(grp[0])
            for j in range(8):
                b = bank1()
                for kc in range(8):
                    mm(ps(b, 0, n), W[:, kc, j * 128:(j + 1) * 128], UT[:, kc, u0:u0 + n], kc == 0, kc == 7,
                       [("slab", si)] + [("UT", s) for s in grp], b)
                copy_op(evac_eng(), QT[:, j, c0:c0 + n], ps(b, 0, n), [("ps", b)], [("QT", s) for s in grp])

    def load_small(T):
        l = T.l
        for kvh in range(2):
            for e_ in range(2):
                v = kvh * 2 + e_
                src = w_in[l, :, 1024 + kvh * 64: 1024 + kvh * 64 + 64].rearrange("(kc p) n -> p kc n", p=128)
                dma("pool", WKK4[:, :, v, e_ * 64:(e_ + 1) * 64], src, ("wkk", v), [], ["WKK4"])
        dma("pool", WV[:, :, :], w_in[l, :, 1152:1280].rearrange("(kc p) n -> p kc n", p=128), "wv", [], ["WV"])
        for g in range(4):
            dma("pool", WMIX[:, :, g, :], w_mix[l, g].rearrange("(kc p) d -> p kc d", p=128), ("wmix", g), [], ["WMIX"])

    def phase_kv_attn(T):
        l = T.l
        for grp in chunks(T.alls):
            u0, n = ut_cols(grp)
            for v in range(4):
                b = bank1()
                for kc in range(8):
                    mm(ps(b, 0, n), WKK4[:, kc, v, :], UT[:, kc, u0:u0 + n], kc == 0, kc == 7,
                       ["WKK4"] + [("UT", s) for s in grp], b)
                copy_op(evac_eng(), KT4[:, v, u0:u0 + n], ps(b, 0, n), [("ps", b)], [("KT4", s) for s in grp])
        for s in T.alls:
            b = bank1()
            for kc in range(8):
                mm(ps(b, 0, 128), UT[:, kc, s * 128:(s + 1) * 128], WV[:, kc, :], kc == 0, kc == 7,
                   ["WV", ("UT", s)], b)
            src = ps(b, 0, 128).rearrange("p (h d) -> p h d", h=2)
            o = T.o_of[s]
            if o < 0:
                P.op("dve", lambda e, s=s, src=src, o=o: e.tensor_scalar(
                    V1[:, s, :, 0:64], src, VALID[:, o + 2:o + 3], None, ALU.mult),
                    reads=[("ps", b), "VALID"], writes=[("V1", s)])
                for h in range(2):
                    P.op("pool", lambda e, s=s, h=h, o=o: e.tensor_copy(V1[:, s, h, 64:65], VALID[:, o + 2:o + 3]),
                         reads=["VALID"], writes=[("V1", s)])
            else:
                copy_op(evac_eng(), V1[:, s, :, 0:64], src, [("ps", b)], [("V1", s)])
                P.op("pool", lambda e, s=s: e.memset(V1[:, s, :, 64:65], 1.0), writes=[("V1", s)])
        P.alias([("ATT", s) for s in T.full], OLD_B1)
        items = []
        for qs in T.full:
            for p in range(8):
                items.append((qs, p))
        pvbank = {}
        ring = [0]

        def emit_qk(qs, p):
            fs = qs - T.base
            kvh = p // 4
            b = bank_sc()
            r = ring[0] % 3
            ring[0] += 1
            inprev = (qs - 1) in T.alls
            for e_ in range(2):
                v = kvh * 2 + e_
                if inprev:
                    kprev = KT4[:, v, (qs - 1) * 128: qs * 128]
                    rprev = ("KT4", qs - 1)
                else:
                    kprev = KTPREV[:, l, v, :]
                    rprev = ("KTPREV", l)
                kown = KT4[:, v, qs * 128:(qs + 1) * 128]
                q = QT[:, p, fs * 128:(fs + 1) * 128]
                mm(ps(b, e_ * 256, 128), kprev, q, True, False, [rprev, ("QT", qs)], b)
                mm(ps(b, e_ * 256 + 128, 128), kown, q, False, False, [("KT4", qs), ("QT", qs)], b)
                mm(ps(b, e_ * 256, 256), I8[:, :], BIAS[:, 2 * p + e_, :], False, True, ["I8", "BIAS"], b)
            P.op("act", lambda e, b=b, r=r: e.activation(PTS[:, r, :], ps(b, 0, 512), AF.Exp, scale=0.125),
                 reads=[("ps", b)], writes=[("PTS", r)])
            return r

        def pv_loc(h):
            return (h // 7, (h % 7) * 65)

        def emit_pv(qs, p, r):
            kvh = p // 4
            inprev = (qs - 1) in T.alls
            if p == 0:
                pvbank[qs] = [5, 6, 7]
            for e_ in range(2):
                h = 2 * p + e_
                bi, off = pv_loc(h)
                b = pvbank[qs][bi]
                if inprev:
                    vprev = V1[:, qs - 1, kvh, :]
                    rprev = ("V1", qs - 1)
                else:
                    vprev = V1PREV[:, l, kvh, :]
                    rprev = ("V1PREV", l)
                mm(ps(b, off, 65), PTS[:, r, e_ * 256: e_ * 256 + 128], vprev, True, False, [("PTS", r), rprev], b)
                mm(ps(b, off, 65), PTS[:, r, e_ * 256 + 128: e_ * 256 + 256], V1[:, qs, kvh, :], False, True,
                   [("PTS", r), ("V1", qs)], b)
            if p == 7:
                fs = qs - T.base
                par = fs % 2
                for bi, (h0, h1) in enumerate([(0, 7), (7, 14), (14, 16)]):
                    b = pvbank[qs][bi]
                    nh = h1 - h0
                    pv = ps(b, 0, nh * 65).rearrange("p (h d) -> p h d", d=65)
                    P.op("dve", lambda e, pv=pv, h0=h0, h1=h1, par=par: e.tensor_tensor(
                        DN[:, par, h0:h1], pv[:, :, 64], ESINK[:, l, h0:h1], ALU.add),
                        reads=[("ps", b), ("ESINK", l)], writes=[("DN", par, bi)])
                    P.op("dve", lambda e, h0=h0, h1=h1, par=par: e.reciprocal(RN[:, par, h0:h1], DN[:, par, h0:h1]),
                         reads=[("DN", par, bi)], writes=[("RN", par, bi)])
                    P.op("dve", lambda e, pv=pv, h0=h0, h1=h1, nh=nh, par=par, fs=fs: e.tensor_tensor(
                        ATT[:, fs, h0 * 64:h1 * 64].rearrange("p (h d) -> p h d", d=64), pv[:, :, 0:64],
                        RN[:, par, h0:h1].unsqueeze(2).to_broadcast([128, nh, 64]), ALU.mult),
                        reads=[("ps", b), ("RN", par, bi)], writes=[("ATT", qs)])

        prev = None
        for (qs, p) in items:
            r = emit_qk(qs, p)
            if prev is not None:
                emit_pv(*prev)
            prev = (qs, p, r)
        emit_pv(*prev)
        last = T.full[-1]
        copy_op("pool", KTPREV[:, l, :, :], KT4[:, :, last * 128:(last + 1) * 128], [("KT4", last)], [("KTPREV", l)])
        copy_op("pool", V1PREV[:, l, :, :], V1[:, last, :, :], [("V1", last)], [("V1PREV", l)])

    def phase_agate(T):
        (si,) = acquire(1)
        W = SLB(si)
        for k, s in enumerate(T.full):
            fs = s - T.base
            b = bank2()
            for half in range(2):
                for kc in range(8):
                    mm(ps(b + half, 0, 512), UT[:, kc, s * 128:(s + 1) * 128], W[:, kc, half * 512:(half + 1) * 512],
                       kc == 0, kc == 7, [("slab", si), ("UT", s)], b + half)
            t = k % 2
            P.op("act", lambda e, b=b, t=t: e.activation(TMPA[:, t, :], PS[:, b * 512:(b + 2) * 512], AF.Silu),
                 reads=[("ps", b), ("ps", b + 1)], writes=[("TMPA", t)])
            P.op("pool", lambda e, fs=fs, t=t: e.tensor_tensor(ATT[:, fs, :], ATT[:, fs, :], TMPA[:, t, :], ALU.mult),
                 reads=[("ATT", s), ("TMPA", t)], writes=[("ATT", s)])

    def gated_proj(T, W1, s1, W2, s2, src, srckey, dst, dstkey, add_to=None, hook=None):
        k = 0
        for grp in chunks(T.full):
            u0, n = ut_cols(grp)
            c0 = T.fcol(grp[0])
            for oc in range(8):
                bm = bank1()
                for kc in range(8):
                    mm(ps(bm, 0, n), W2[:, kc, oc * 128:(oc + 1) * 128], UT[:, kc, u0:u0 + n], kc == 0, kc == 7,
                       [("slab", s2)] + [("UT", s) for s in grp], bm)
                ba = bank1()
                for kc in range(8):
                    mm(ps(ba, 0, n), W1[:, kc, oc * 128:(oc + 1) * 128], src[:, kc, c0:c0 + n], kc == 0, kc == 7,
                       [("slab", s1)] + [(srckey, s) for s in grp], ba)
                t = k % 2
                k += 1
                P.op("act", lambda e, bm=bm, n=n, t=t: e.activation(TMPT[:, t, 0:n], ps(bm, 0, n), AF.Tanh, scale=0.5),
                     reads=[("ps", bm)], writes=[("TMPT", t)])
                if add_to is None:
                    P.op("dve", lambda e, ba=ba, n=n, t=t, oc=oc, c0=c0: e.scalar_tensor_tensor(
                        dst[:, oc, c0:c0 + n], TMPT[:, t, 0:n], 1.0, ps(ba, 0, n), ALU.add, ALU.mult),
                        reads=[("ps", ba), ("TMPT", t)], writes=[(dstkey, s, oc) for s in grp])
                else:
                    P.op("dve", lambda e, ba=ba, n=n, t=t: e.scalar_tensor_tensor(
                        TMP2[:, t, 0:n], TMPT[:, t, 0:n], 1.0, ps(ba, 0, n), ALU.add, ALU.mult),
                        reads=[("ps", ba), ("TMPT", t)], writes=[("TMP2", t)])
                    P.op("pool", lambda e, n=n, t=t, oc=oc, c0=c0: e.tensor_tensor(
                        dst[:, oc, c0:c0 + n], add_to[:, oc, c0:c0 + n], TMP2[:, t, 0:n], ALU.add),
                        reads=[("TMP2", t)] + [(dstkey, s, oc) for s in grp], writes=[(dstkey, s, oc) for s in grp])
                if hook is not None and oc == 3 and grp[-1] == T.full[-1]:
                    hook()

    def phase_aproj(T):
        s1, s2 = acquire(2)
        P.alias([("AT", s) for s in T.full], OLD_B0)
        for s in T.full:
            fs = s - T.base
            b = bank1()
            for kc in range(8):
                P.op("pe", lambda e, b=b, kc=kc, fs=fs: e.transpose(
                    PSB[:, b * 1024 + kc * 128: b * 1024 + (kc + 1) * 128], ATT[:, fs, kc * 128:(kc + 1) * 128],
                    IDENTB[:, :]), reads=[("ATT", s), "IDENTB"], writes=[("ps", b)])
            P.op("dve", lambda e, b=b, fs=fs: e.tensor_copy(
                AT[:, :, fs * 128:(fs + 1) * 128], PSB[:, b * 1024:(b + 1) * 1024].rearrange("p (k t) -> p k t", k=8)),
                reads=[("ps", b)], writes=[("AT", s)])
        P.alias([("ZA", s, oc) for s in T.full for oc in range(8)], OLD_B2)
        gated_proj(T, SLB(s1), s1, SLB(s2), s2, AT, "AT", ZA, "ZA")

    def phase_pool(T):
        l = T.l
        s1, s2 = acquire(2)
        Wpin, Wpg = SLB(s1), SLB(s2)
        P.alias([("PIN", s) for s in T.alls], OLD_PIN)
        for k, s in enumerate(T.alls):
            b = bank2()
            for half in range(2):
                for kc in range(8):
                    mm(ps(b + half, 0, 512), UT[:, kc, s * 128:(s + 1) * 128], Wpin[:, kc, half * 512:(half + 1) * 512],
                       kc == 0, kc == 7, [("slab", s1), ("UT", s)], b + half)
            o = T.o_of[s]
            src = PS[:, b * 512:(b + 2) * 512]
            rd = [("ps", b), ("ps", b + 1)]
            if o < 0:
                P.op("dve", lambda e, s=s, src=src, o=o: e.tensor_scalar(
                    PIN[:, s, :], src, VALID[:, o + 2:o + 3], None, ALU.mult),
                    reads=rd + ["VALID"], writes=[("PIN", s)])
            else:
                copy_op(evac_eng(), PIN[:, s, :], src, rd, [("PIN", s)])
        P.alias([("POOLED", s, hb) for s in T.full for hb in range(2)], OLD_B0)
        for s in T.full:
            fs = s - T.base
            if (s - 1) in T.alls:
                prev = PIN[:, s - 1, :]
                rprev = ("PIN", s - 1)
            else:
                prev = PINPREV[:, l, :]
                rprev = ("PINPREV", l)
            own = 2 if T.o_of[s] == 0 else 1
            for hb in range(2):
                b = bank1()
                for q in range(4):
                    fc = hb * 4 + q
                    g = fc // 2
                    mm(ps(b, q * 128, 128), prev[:, fc * 128:(fc + 1) * 128], BAND[:, 0, g, :], True, False,
                       [rprev, "BAND"], b)
                    mm(ps(b, q * 128, 128), PIN[:, s, fc * 128:(fc + 1) * 128], BAND[:, own, g, :], False, True,
                       [("PIN", s), "BAND"], b)
                copy_op(evac_eng(), POOLED[:, hb * 4:hb * 4 + 4, fs * 128:(fs + 1) * 128],
                        ps(b, 0, 512).rearrange("p (q t) -> p q t", q=4), [("ps", b)], [("POOLED", s, hb)])
        last = T.full[-1]
        copy_op("pool", PINPREV[:, l, :], PIN[:, last, :], [("PIN", last)], [("PINPREV", l)])
        P.alias([("PT", s) for s in T.full], OLD_B1)
        k = 0
        for grp in chunks(T.full):
            u0, n = ut_cols(grp)
            c0 = T.fcol(grp[0])
            for oc in range(8):
                g = oc // 2
                bg = bank1()
                for kc in range(8):
                    mm(ps(bg, 0, n), Wpg[:, kc, oc * 128:(oc + 1) * 128], UT[:, kc, u0:u0 + n], kc == 0, kc == 7,
                       [("slab", s2)] + [("UT", s) for s in grp], bg)
                bm = bank1()
                for kc2 in range(2):
                    mm(ps(bm, 0, n), WMIX[:, kc2, g, (oc % 2) * 128:(oc % 2 + 1) * 128],
                       POOLED[:, 2 * g + kc2, c0:c0 + n], kc2 == 0, kc2 == 1,
                       ["WMIX"] + [("POOLED", s, (2 * g + kc2) // 4) for s in grp], bm)
                t = k % 2
                k += 1
                P.op("act", lambda e, bg=bg, n=n, t=t: e.activation(TMPA[:, t, 0:n], ps(bg, 0, n), AF.Silu),
                     reads=[("ps", bg)], writes=[("TMPA", t)])
                P.op("dve", lambda e, bm=bm, n=n, t=t, oc=oc, c0=c0: e.scalar_tensor_tensor(
                    PT[:, oc, c0:c0 + n], ps(bm, 0, n), PSC[:, l, oc:oc + 1], TMPA[:, t, 0:n], ALU.mult, ALU.mult),
                    reads=[("ps", bm), ("TMPA", t), "PSC"], writes=[("PT", s) for s in grp])

    def phase_pproj(T):
        l = T.l
        s1, s2 = acquire(2)
        P.alias(["LNT"], OLD_B0)
        dma("sp", LNT[:, 0, :], bcast(ln_g, l * 1024, 1024), "lnt", [], ["LNT"])
        dma("sp", LNT[:, 1, :], bcast(ln_b, l * 1024, 1024), "lnt", [], ["LNT"])
        so = st["used"] % NSLAB

        def scale_wout():
            Wo = SLB(so)
            P.op("pool", lambda e: e.tensor_tensor(
                Wo[:, :, :], Wo[:, :, :], G1[:, l, :].unsqueeze(1).to_broadcast([128, 8, 1024]), ALU.mult),
                reads=[("slab", so), ("G1", l)], writes=[("slab", so)])

        gated_proj(T, SLB(s1), s1, SLB(s2), s2, PT, "PT", ZA, "ZA", add_to=ZA)

    def phase_out(T):
        l = T.l
        (si,) = acquire(1)
        W = SLB(si)
        P.alias(["T1a", "T1b"], OLD_B1)
        nb = len(T.full)

        def stage1(k):
            s = T.full[k]
            fs = s - T.base
            par = k % 2
            tk = "T1a" if par == 0 else "T1b"
            xs = T.xp[s]
            b = bank2()
            for half in range(2):
                for kc in range(8):
                    mm(ps(b + half, 0, 512), ZA[:, kc, fs * 128:(fs + 1) * 128], W[:, kc, half * 512:(half + 1) * 512],
                       kc == 0, kc == 7, [("slab", si)] + [("ZA", s, oc) for oc in range(8)], b + half)
            P.op("dve", lambda e, b=b, par=par: e.tensor_tensor(
                T1[:, par, :], PS[:, b * 512:(b + 2) * 512], G1[:, l, :], ALU.mult),
                reads=[("ps", b), ("ps", b + 1), ("G1", l)], writes=[tk])
            P.op("pool", lambda e, par=par, xs=xs: e.tensor_tensor(T1[:, par, :], T1[:, par, :], X[:, xs, :], ALU.add),
                 reads=[tk, ("X", xs)], writes=[tk])

        def stage2(k):
            s = T.full[k]
            par = k % 2
            tk = "T1a" if par == 0 else "T1b"
            xs = T.xp[s]
            for hh in range(2):
                P.op("dve", lambda e, par=par, hh=hh: e.bn_stats(ST[:, par, hh, :], T1[:, par, hh * 512:(hh + 1) * 512]),
                     reads=[tk], writes=[("ST", par, hh)])
            P.op("dve", lambda e, par=par: e.bn_aggr(MV[:, par, :], ST[:, par, :, :].rearrange("p a b -> p (a b)")),
                 reads=[("ST", par, 0), ("ST", par, 1)], writes=[("MV", par)])
            P.op("act", lambda e, par=par: e.activation(
                SD[:, par, :], MV[:, par, 1:2], AF.Sqrt, bias=EPSC[:, 0:1], scale=1.0),
                reads=[("MV", par), "EPSC"], writes=[("SD", par)])
            P.op("dve", lambda e, par=par: e.reciprocal(RSTD[:, par, :], SD[:, par, :]),
                 reads=[("SD", par)], writes=[("RSTD", par)])
            P.op("dve", lambda e, par=par: e.scalar_tensor_tensor(
                NMR[:, par, :], MV[:, par, 0:1], -1.0, RSTD[:, par, :], ALU.mult, ALU.mult),
                reads=[("MV", par), ("RSTD", par)], writes=[("NMR", par)])
            P.op("act", lambda e, par=par, xs=xs: e.activation(
                X[:, xs, :], T1[:, par, :], AF.Identity, bias=NMR[:, par, :], scale=RSTD[:, par, :]),
                reads=[tk, ("NMR", par), ("RSTD", par)], writes=[("X", xs)])

        def stage3(k):
            s = T.full[k]
            xs = T.xp[s]
            P.op("dve", lambda e, xs=xs: e.tensor_tensor(X[:, xs, :], X[:, xs, :], LNT[:, 0, :], ALU.mult),
                 reads=[("X", xs), "LNT"], writes=[("X", xs)])
            P.op("pool", lambda e, xs=xs: e.tensor_tensor(X[:, xs, :], X[:, xs, :], LNT[:, 1, :], ALU.add),
                 reads=[("X", xs), "LNT"], writes=[("X", xs)])
            if l == DEPTH - 1:
                o = T.o_of[s]
                dma("sp", out_d[o], X[:, xs, :], ("out", xs), [("X", xs)], [("OUT", o)])
                if T.idx < 3 and k < 2:
                    nT = Tile(T.idx + 1, 0)
                    p = nT.xp[3 + k]
                    dma("sp", X[:, p, :], STG[:, k, :], ("x", p), ["STG%d" % k], [("X", p)])

        for step in range(nb + 2):
            if step < nb:
                stage1(step)
            if 0 <= step - 1 < nb:
                stage2(step - 1)
            if 0 <= step - 2 < nb:
                stage3(step - 2)

    tapd = {}

    def tap(name, ap, shape, dt):
        t = nc.dram_tensor("tap_" + name, list(shape), dt, kind="ExternalOutput").ap()
        tapd[name] = t
        return t

    for ti, T in enumerate(tiles):
        load_small(T)
        phase_ut(T)
        if T.l == 1 and T.idx < 3:
            load_x(Tile(T.idx + 1, 0), [1, 2])
        phase_q(T)
        if ti == 0:
            ada_gate(0)
            ada_gate(1)
        phase_kv_attn(T)
        if ti == 0:
            ada_cols(1, 0)
        phase_agate(T)
        if ti == 0:
            ada_cols(1, 1)
        phase_aproj(T)
        phase_pool(T)
        if T.l == 1 and T.idx < 3:
            nT = Tile(T.idx + 1, 0)
            P.alias(["STG0", "STG1"], OLD_PIN)
            for j in range(2):
                dma("sp", STG[:, j, :], x_in[nT.o_of[3 + j] + 2], ("stg", j), [], ["STG%d" % j])
        phase_pproj(T)
        phase_out(T)
        if ("x1", ti) in taps:
            t = tap("x1_%d" % ti, None, [128, 6, 1024], F32)
            dma("sp", t, X[:, :, :], ("tap", ti), [("X", s) for s in range(6)], [("TAP", ti)])

    P.emit()
    return nc


def _t5_bucket(dist):
    max_exact = 16
    d = np.maximum(dist, 1).astype(np.float32)
    large = max_exact + (np.log(d / max_exact) / math.log(128 / max_exact) * (32 - max_exact)).astype(np.int32)
    large = np.minimum(large, 31)
    return np.where(dist < max_exact, dist, large)


def _structure():
    k = np.arange(128)[:, None]
    col = np.arange(256)[None, :]
    q = col % 128
    dist = np.where(col < 128, q + 128 - k, q - k)
    valid = (dist >= 0) & (dist < 128)
    bucket = _t5_bucket(np.maximum(dist, 0))
    msk = np.where(valid, 0.0, NEG).astype(np.float32)
    s = np.arange(128)[:, None]
    t = np.arange(128)[None, :]
    bands = np.zeros((3, 4, 128, 128), np.float32)
    for g, w in enumerate(POOL_WINDOWS):
        bands[0, g] = np.where(s - 128 >= t - w + 1, 1.0 / w, 0.0)
        own = np.where((s <= t) & (s >= t - w + 1), 1.0 / w, 0.0) - (s == t)
        bands[1, g] = own
        cnt = np.minimum(t + 1, w).astype(np.float32)
        bands[2, g] = np.where((s <= t) & (s >= t - w + 1), 1.0 / cnt, 0.0) - (s == t)
    return bucket, valid, msk, bands


_NC_CACHE = {}


def _host_inputs(inputs):
    x = np.asarray(inputs["x"], np.float32)
    c = np.asarray(inputs["c"], np.float32)
    rel_bias = np.asarray(inputs["rel_bias"], np.float32)
    b_ada = np.asarray(inputs["b_ada"], np.float32)
    pool_scale = np.asarray(inputs["pool_scale"], np.float32)
    bucket, valid, msk, bands = _structure()
    rbg = rel_bias[bucket]
    rbg = np.where(valid[:, :, None], rbg, np.float32(0.0))
    rbg = np.ascontiguousarray(np.transpose(rbg, (0, 2, 1)))
    ident = np.eye(128, dtype=np.float32)
    i8 = np.eye(128, dtype=np.float32) * 8.0
    bada_fm = np.ascontiguousarray(np.transpose(b_ada[:, :2048].reshape(2, 16, 128), (2, 0, 1)))
    bada_g = np.ascontiguousarray(b_ada[:, 2048:3072])
    pscale_fm = np.ascontiguousarray(np.transpose(pool_scale.reshape(2, 8, 128), (2, 0, 1)))
    shared = dict(
        w_ada=np.asarray(inputs["w_ada"], np.float32), bada_fm=bada_fm, bada_g=bada_g,
        w_in=np.asarray(inputs["w_in"], np.float32), w_ap=np.asarray(inputs["w_attn_proj"], np.float32),
        w_pp=np.asarray(inputs["w_pool_proj"], np.float32), w_out=np.asarray(inputs["w_out"], np.float32),
        w_mix=np.asarray(inputs["w_pool_mix"], np.float32), pscale_fm=pscale_fm,
        sinks=np.asarray(inputs["sinks"], np.float32), ln_g=np.asarray(inputs["ln_gain"], np.float32),
        ln_b=np.asarray(inputs["ln_bias"], np.float32), rbg=rbg, msk=msk, ident=ident, i8=i8,
    )
    in_maps = []
    for core in range(8):
        b, qd = core // 4, core % 4
        t0 = qd * 2048
        xin = np.zeros((18, 128, 1024), np.float32)
        if qd == 0:
            xin[2:] = x[b, 0:2048].reshape(16, 128, 1024)
        else:
            xin[:] = x[b, t0 - 256:t0 + 2048].reshape(18, 128, 1024)
        bv = np.full((128, 2), 0.0 if qd == 0 else 1.0, np.float32)
        bd = bands.copy()
        if qd != 0:
            bd[2] = bd[1]
        m = dict(shared)
        m.update(x_in=xin, c_lay=np.ascontiguousarray(np.repeat(c[b].reshape(8, 128).T[:, :, None], 2, axis=2)), blkvalid=bv, bands=bd)
        in_maps.append(m)
    return in_maps


def kernel(**inputs):
    in_maps = _host_inputs(inputs)
    if "nc" not in _NC_CACHE:
        _NC_CACHE["nc"] = build_program()
    nc = _NC_CACHE["nc"]
    res = run_bass_kernel_spmd(nc, in_maps, core_ids=list(range(8)))
    out = np.zeros((2, 8192, 1024), np.float32)
    for core in range(8):
        b, qd = core // 4, core % 4
        out[b, qd * 2048:(qd + 1) * 2048] = np.asarray(res.results[core]["out"]).reshape(2048, 1024)
    return out
```
